# Optimizing a Trainium2 kernel written in Bass

```python
import jax, jax.numpy as jnp
from jax import lax
import numpy as np

D_MODEL = 1024
BATCH = 8
SEQ = 4096
DEPTH = 2
DEC_BATCH = 16
DEC_SEQ = 32
PAST_LEN = 2048

CHUNK = 64
N_MIXERS = 2
N_A = (DEPTH + 1) // 2
N_B = DEPTH // 2
HG_EXPAND = 128
HG_HEADS = D_MODEL // HG_EXPAND
HG_DK = HG_EXPAND
HG_DV = D_MODEL // HG_HEADS
RECUR_BLOCK = 16
CONV_W = 3
D_FF = 4 * D_MODEL
EPS = 1e-6

kernel_name = "hgrn2_shortconv_hybrid_stream_step"


def rmsnorm(x, g):
    xf = x.astype(jnp.float32)
    y = xf * lax.rsqrt(jnp.mean(xf * xf, axis=-1, keepdims=True) + EPS)
    return (y * g.astype(jnp.float32)).astype(x.dtype)


def hgrn2_recurrence(q, log_f, k, v, s0):
    B, T = q.shape[0], q.shape[1]
    L = RECUR_BLOCK
    n = -(-T // L)
    pad = n * L - T
    padw = ((0, 0), (0, pad), (0, 0), (0, 0))
    q, log_f, k, v = [jnp.pad(a.astype(jnp.float32), padw) for a in (q, log_f, k, v)]

    def blocks(a):
        return a.reshape(B, n, L, HG_HEADS, a.shape[-1]).transpose(1, 0, 3, 2, 4)

    qb, gb, kb, vb = blocks(q), blocks(log_f), blocks(k), blocks(v)
    mask = jnp.tril(jnp.ones((L, L), dtype=bool))

    def step(S, inp):
        qj, gj, kj, vj = inp
        b = jnp.cumsum(gj, axis=-2)
        bL = b[..., -1:, :]
        qt = qj * jnp.exp(b)
        kt = kj * jnp.exp(-b)
        A = jnp.where(mask, jnp.einsum('bhtk,bhsk->bhts', qt, kt), 0.0)
        o = jnp.einsum('bhtk,bhkv->bhtv', qt, S) + jnp.einsum('bhts,bhsv->bhtv', A, vj)
        S = jnp.exp(bL[..., 0, :])[..., None] * S + jnp.einsum('bhsk,bhsv->bhkv', kj * jnp.exp(bL - b), vj)
        return S, o

    S, o = lax.scan(step, s0.astype(jnp.float32), (qb, gb, kb, vb))
    o = o.transpose(1, 0, 3, 2, 4).reshape(B, n * L, HG_HEADS, HG_DV)[:, :T]
    return o, S


def hgrn2_mixer(h, s0, w_in, lb, g_norm, w_out):
    B, T, _ = h.shape
    proj = h @ w_in
    q, fl, vi, g = jnp.split(proj, 4, axis=-1)
    lbf = lb.astype(jnp.float32)
    f = lbf + (1.0 - lbf) * jax.nn.sigmoid(fl.astype(jnp.float32))
    log_f = jnp.log(f)
    kk = 1.0 - f
    heads = lambda a, d: a.reshape(B, T, HG_HEADS, d)
    o, S = hgrn2_recurrence(heads(q, HG_DK), heads(log_f, HG_DK), heads(kk, HG_DK), heads(vi, HG_DV), s0)
    o = o * lax.rsqrt(jnp.mean(o * o, axis=-1, keepdims=True) + EPS)
    o = o.reshape(B, T, D_MODEL) * g_norm.astype(jnp.float32) * jax.nn.silu(g.astype(jnp.float32))
    return o.astype(h.dtype) @ w_out, S.astype(h.dtype)


def short_conv_mixer(h, buf, w_in, conv_w, w_out):
    T = h.shape[1]
    proj = h @ w_in
    bg, cg, u = jnp.split(proj, 3, axis=-1)
    z = cg * u
    zc = jnp.concatenate([buf.astype(z.dtype), z], axis=1)
    conv = zc[:, 0:T] * conv_w[0]
    for j in range(1, CONV_W):
        conv = conv + zc[:, j:j + T] * conv_w[j]
    y = bg * conv
    return y @ w_out, zc[:, -(CONV_W - 1):]


def trunk(x, hg_state, cv_state, norm_mix, norm_mlp, norm_final, w_in_hgrn, lb_hgrn,
          g_norm_hgrn, w_out_hgrn, w_in_conv, conv_w, w_out_conv, w_up, w_down):
    lbs = jnp.cumsum(jax.nn.softmax(lb_hgrn.astype(jnp.float32), axis=0), axis=0)
    new_hg, new_cv = [], []
    h = x
    for i in range(DEPTH):
        a = rmsnorm(h, norm_mix[i])
        k = i // N_MIXERS
        if i % N_MIXERS == 0:
            y, s = hgrn2_mixer(a, hg_state[k], w_in_hgrn[k], lbs[k], g_norm_hgrn[k], w_out_hgrn[k])
            new_hg.append(s)
        else:
            y, s = short_conv_mixer(a, cv_state[k], w_in_conv[k], conv_w[k], w_out_conv[k])
            new_cv.append(s)
        h = h + y
        m = rmsnorm(h, norm_mlp[i])
        h = h + jnp.square(jax.nn.relu(m @ w_up[i])) @ w_down[i]
    return rmsnorm(h, norm_final), jnp.stack(new_hg), jnp.stack(new_cv)


def setup_inputs(seed: int = 0) -> dict:
    key = jax.random.key(seed)
    ks = jax.random.split(key, 16)
    f32 = jnp.float32
    nrm = lambda k, shape, s: jax.random.normal(k, shape, f32) * s
    return {
        "x_prompt": nrm(ks[0], (BATCH, SEQ, D_MODEL), 1.0),
        "x_sample": nrm(ks[1], (DEC_BATCH, DEC_SEQ, D_MODEL), 1.0),
        "state_hgrn": nrm(ks[2], (N_A, DEC_BATCH, HG_HEADS, HG_DK, HG_DV), 0.5),
        "state_conv": nrm(ks[3], (N_B, DEC_BATCH, CONV_W - 1, D_MODEL), 1.0),
        "norm_mix": 1.0 + nrm(ks[4], (DEPTH, D_MODEL), 0.02),
        "norm_mlp": 1.0 + nrm(ks[5], (DEPTH, D_MODEL), 0.02),
        "norm_final": 1.0 + nrm(ks[6], (D_MODEL,), 0.02),
        "w_in_hgrn": nrm(ks[7], (N_A, D_MODEL, 4 * D_MODEL), D_MODEL ** -0.5),
        "lb_hgrn": nrm(ks[8], (N_A + 1, HG_HEADS * HG_DK), 0.1),
        "g_norm_hgrn": 1.0 + nrm(ks[9], (N_A, D_MODEL), 0.02),
        "w_out_hgrn": nrm(ks[10], (N_A, D_MODEL, D_MODEL), D_MODEL ** -0.5),
        "w_in_conv": nrm(ks[11], (N_B, D_MODEL, 3 * D_MODEL), D_MODEL ** -0.5),
        "conv_w": nrm(ks[12], (N_B, CONV_W, D_MODEL), CONV_W ** -0.5),
        "w_out_conv": nrm(ks[13], (N_B, D_MODEL, D_MODEL), D_MODEL ** -0.5),
        "w_up": nrm(ks[14], (DEPTH, D_MODEL, D_FF), D_MODEL ** -0.5),
        "w_down": nrm(ks[15], (DEPTH, D_FF, D_MODEL), D_FF ** -0.5),
    }


def reference(x_prompt, x_sample, state_hgrn, state_conv, norm_mix, norm_mlp, norm_final,
              w_in_hgrn, lb_hgrn, g_norm_hgrn, w_out_hgrn, w_in_conv, conv_w, w_out_conv,
              w_up, w_down):
    bp = x_prompt.shape[0]
    hg0 = jnp.zeros((N_A, bp, HG_HEADS, HG_DK, HG_DV), x_prompt.dtype)
    cv0 = jnp.zeros((N_B, bp, CONV_W - 1, D_MODEL), x_prompt.dtype)
    y_prompt, state_hgrn_prompt, state_conv_prompt = trunk(
        x_prompt, hg0, cv0, norm_mix, norm_mlp, norm_final, w_in_hgrn, lb_hgrn,
        g_norm_hgrn, w_out_hgrn, w_in_conv, conv_w, w_out_conv, w_up, w_down)
    y_sample, state_hgrn_sample, state_conv_sample = trunk(
        x_sample, state_hgrn, state_conv, norm_mix, norm_mlp, norm_final, w_in_hgrn, lb_hgrn,
        g_norm_hgrn, w_out_hgrn, w_in_conv, conv_w, w_out_conv, w_up, w_down)
    return (y_prompt, y_sample, state_hgrn_prompt, state_conv_prompt, state_hgrn_sample, state_conv_sample)
```

```python
import numpy as np
from contextlib import ExitStack
import concourse.bass as bass
import concourse.mybir as mybir
from concourse.bass_utils import run_bass_kernel_spmd

F32 = mybir.dt.float32
BF16 = mybir.dt.bfloat16
AF = mybir.ActivationFunctionType
ALU = mybir.AluOpType

D = 1024
KC = 8
TP = 1024
TC = TP + 64
EPS = 1e-6
NSLOT = 4


class Sem:
    def __init__(self, h):
        self.h = h
        self.v = 0


class Res:
    __slots__ = ("w", "r", "name")

    def __init__(self, name="", pre=None):
        self.w = list(pre) if pre else []
        self.r = {}
        self.name = name


class Q:
    def __init__(self, name, eng, sem):
        self.name = name
        self.eng = eng
        self.sem = sem
        self.seen = {}
        self.lag = 1 << 30


def _deps(reads, writes):
    d = []
    for r in reads:
        d.extend(r.w)
    for w in writes:
        d.extend(w.w)
        d.extend(w.r.values())
    return d


CLK = {}


def _merge(seen, clk):
    for k, v in clk.items():
        if seen.get(k, 0) < v:
            seen[k] = v


def _wait(q, deps, fold=False):
    best = {}
    for (sm, v) in deps:
        if sm is q.sem and v <= sm.v - q.lag:
            continue
        if id(sm) not in best or best[id(sm)][1] < v:
            best[id(sm)] = (sm, v)
    need = []
    for (sm, v) in sorted(best.values(), key=lambda e: -len(CLK.get((id(e[0]), e[1]), ()))):
        if q.seen.get(id(sm), 0) >= v:
            continue
        q.seen[id(sm)] = v
        clk = CLK.get((id(sm), v))
        if clk is not None:
            _merge(q.seen, clk)
        need.append((sm, v))
    last = None
    if fold and need:
        last = need.pop()
    for (sm, v) in need:
        q.eng.wait_ge(sm.h, v)
    return last


def _stamp(q, ev):
    clk = dict(q.seen)
    clk[id(q.sem)] = max(clk.get(id(q.sem), 0), q.sem.v if ev[0] is q.sem else clk.get(id(q.sem), 0))
    clk[id(ev[0])] = max(clk.get(id(ev[0]), 0), ev[1])
    CLK[(id(ev[0]), ev[1])] = clk


def _record(ev, reads, writes):
    sm = ev[0]
    for r in reads:
        r.r[id(sm)] = ev
    for w in writes:
        w.w = [ev]
        w.r = {}


def op(q, fn, reads=(), writes=(), dsem=None):
    last = _wait(q, _deps(reads, writes), fold=True)
    ins = fn(q.eng)
    if last is not None:
        ins._wait_ge(last[0].h, last[1])
    if dsem is not None:
        dsem.v += 16
        ins.then_inc(dsem.h, 16)
        ev = (dsem, dsem.v)
    else:
        q.sem.v += 1
        ins.then_inc(q.sem.h, 1)
        ev = (q.sem, q.sem.v)
    _stamp(q, ev)
    _record(ev, reads, writes)
    return ev


def build(NT=4):
    CLK.clear()
    SEQ = NT * TP
    nc = bass.Bass("TRN2", target_bir_lowering=False)
    dr = lambda n, s, k: nc.dram_tensor(n, s, F32, kind=k).ap()
    xp = dr("xp", [SEQ, D], "ExternalInput")
    xs = dr("xs", [64, D], "ExternalInput")
    s_hg = dr("s_hg", [2, 8, 128, 128], "ExternalInput")
    s_cv = dr("s_cv", [2, 2, D], "ExternalInput")
    norm_mix = dr("norm_mix", [2, D], "ExternalInput")
    norm_mlp = dr("norm_mlp", [2, D], "ExternalInput")
    norm_final = dr("norm_final", [1, D], "ExternalInput")
    w_in_hgrn = dr("w_in_hgrn", [D, 4 * D], "ExternalInput")
    lb_hgrn = dr("lb_hgrn", [2, D], "ExternalInput")
    g_norm_hgrn = dr("g_norm_hgrn", [1, D], "ExternalInput")
    w_out_hgrn = dr("w_out_hgrn", [D, D], "ExternalInput")
    w_in_conv = dr("w_in_conv", [D, 3 * D], "ExternalInput")
    conv_w = dr("conv_w", [3, D], "ExternalInput")
    w_out_conv = dr("w_out_conv", [D, D], "ExternalInput")
    w_up = dr("w_up", [2, D, 4 * D], "ExternalInput")
    w_down = dr("w_down", [2, 4 * D, D], "ExternalInput")
    yp = dr("yp", [SEQ, D], "ExternalOutput")
    ys = dr("ys", [64, D], "ExternalOutput")
    hgp = dr("hgp", [8, 128, 128], "ExternalOutput")
    cvp = dr("cvp", [2, D], "ExternalOutput")
    hgs = dr("hgs", [2, 8, 128, 128], "ExternalOutput")
    cvs = dr("cvs", [2, 2, D], "ExternalOutput")

    with ExitStack() as es:
        def sb(name, shape, dt, stack=es):
            return stack.enter_context(nc.sbuf_tensor(name, shape, dt))

        def newsem(name):
            return Sem(es.enter_context(nc.semaphore(name)))

        h = sb("h", [128, 9, D], F32)
        aFM = sb("aFM", [128, KC, TC], BF16)
        hid = sb("hid", [128, KC, TC], BF16)
        wring = [sb("wring%d" % i, [128, 4096], BF16) for i in range(NSLOT)]
        ident = sb("ident", [128, 128], BF16)
        ones_bf = sb("ones_bf", [128, 128], BF16)
        tri4 = sb("tri4", [128, 4, 128], BF16)
        msk_s = sb("msk_s", [128, 64], F32)
        smask = sb("smask", [128, TC], F32)
        eps_t = sb("eps_t", [128, 1], F32)
        lbraw = sb("lbraw", [128, 2, 8], F32)
        lbm1 = sb("lbm1", [128, 8], F32)
        oml = sb("oml", [128, 8], F32)
        gnorm = sb("gnorm", [128, 8], F32)
        cwt = sb("cwt", [128, 3, 8], F32)
        gam = [sb("gam%d" % i, [128, D], F32) for i in range(2)]
        NATM = 2
        aTM = [sb("aTM%d" % i, [128, D], BF16) for i in range(NATM)]
        ss2 = [sb("ss%d" % i, [128, 9], F32) for i in range(2)]
        rstd2 = [sb("rstd%d" % i, [128, 9], F32) for i in range(2)]
        ss = ss2[0]
        rstd = rstd2[0]
        NYO = 3
        yo = [sb("yo%d" % i, [128, D], F32) for i in range(NYO)]
        identf = yo[0][:, 0:128]
        trif = yo[1][:, 0:512].rearrange("p (a b) -> p a b", a=4)
        Ucarry = sb("Ucarry", [128, 8, 128], F32)
        ebLcarry = sb("ebLcarry", [128, 8], F32)
        zcarry = sb("zcarry", [128, 2, 8], F32)
        rl = [sb("rl%d" % i, [128, 512], BF16) for i in range(2)]
        shist = sb("shist", [128, 2, 2, 8], F32)
        rowmask = sb("rowmask", [128, 2], F32)

        def ps(name, shape, dt):
            return es.enter_context(nc.psum_tensor(name, shape, dt))
        NB = 8
        mm_banks = [ps("mm%d" % i, [128, 512], F32) for i in range(NB)]
        mm_res = [Res("mm%d" % i) for i in range(NB)]
        mm_ctr = [0]

        held = set()

        def mm_next(hold=False):
            for _ in range(NB + 1):
                i = mm_ctr[0] % NB
                mm_ctr[0] += 1
                if i not in held:
                    break
            else:
                raise RuntimeError("no free PSUM bank")
            assert i not in held
            if hold:
                held.add(i)
            return mm_banks[i], mm_res[i]

        def mm_release(res):
            held.discard(mm_res.index(res))

        def v4(bank):
            return bank[:, :].rearrange("p (a b) -> p a b", a=4)

        def tp_next(hold=False):
            bank, r = mm_next(hold)
            return bank[:, :].bitcast(BF16).rearrange("p (a b) -> p a b", a=8), r

        PE = Q("pe", nc.tensor, newsem("s_pe"))
        ACT = Q("act", nc.scalar, newsem("s_act"))
        DVE = Q("dve", nc.vector, newsem("s_dve"))
        POOL = Q("pool", nc.gpsimd, newsem("s_pool"))
        SP = Q("sp", nc.sync, newsem("s_sp"))
        POOL.lag = 1 << 30
        PE.lag = 0
        queues = [PE, ACT, DVE, POOL, SP]
        slot_sem = [newsem("s_slot%d" % i) for i in range(NSLOT)]
        h_sem = [newsem("s_h%d" % i) for i in range(9)]
        gam_sem = [newsem("s_gam%d" % i) for i in range(2)]
        yo_sem = [newsem("s_yo%d" % i) for i in range(NYO)]
        misc_sem = newsem("s_misc")
        s0_sem = [newsem("s_s0%d" % i) for i in range(2)]
        out_sem = newsem("s_out")

        es.enter_context(nc.Block())

        def pe_group(mms, reads, writes):
            last = _wait(PE, _deps(reads, writes), fold=True)
            ins = None
            for m in mms:
                ins = m(PE.eng)
                if last is not None:
                    ins._wait_ge(last[0].h, last[1])
                    last = None
            PE.sem.v += 1
            ins.then_inc(PE.sem.h, 1)
            ev = (PE.sem, PE.sem.v)
            _stamp(PE, ev)
            _record(ev, reads, writes)
            return ev

        h_res = [Res("h%d" % b) for b in range(9)]
        a_res = [Res("a%d" % b) for b in range(9)]
        hid_res = [Res("hid%d" % c) for c in range(3)]
        slot_res = [Res("slot%d" % i) for i in range(NSLOT)]
        gam_res = [Res("gam%d" % i) for i in range(2)]
        aTM_res = [Res("aTM%d" % i) for i in range(NATM)]
        yo_res = [Res("yo%d" % i) for i in range(NYO)]
        rl_res = [Res("rl%d" % i) for i in range(2)]
        sqj_res = Res("sqj")
        ss_res = Res("ss")
        rstd_res = Res("rstd")
        ucarry_res = [Res("uc%d" % i) for i in range(8)]
        eblc_res = [Res("ec%d" % i) for i in range(8)]
        zcarry_res = [Res("zc%d" % i) for i in range(8)]

        early = {}

        def early_loads():
            for b in range(9):
                ntok = 128 if b < 8 else 64
                src = xp[b * 128:(b + 1) * 128, :] if b < 8 else xs[:, :]
                op(SP, lambda e, b=b, ntok=ntok, src=src: e.dma_start(out=h[0:ntok, b, :], in_=src),
                   writes=[h_res[b]], dsem=h_sem[b])
            op(SP, lambda e: e.dma_start(out=gam[0][:], in_=norm_mix[0:1, :].broadcast_to([128, D])),
               writes=[gam_res[0]], dsem=gam_sem[0])
            early["gi"] = 0

        setup = []
        c_idf, c_tri, c_msk, c_sm, c_lb, c_oml, c_lbraw = [Res(n) for n in ("idf", "tri", "msk", "sm", "lb", "oml", "lbraw")]
        setup.append(op(POOL, lambda e: e.memset(identf, 0.0), writes=[c_idf]))
        setup.append(op(POOL, lambda e: e.affine_select(
            identf, identf, pattern=[[-1, 128]], compare_op=ALU.not_equal,
            fill=1.0, base=0, channel_multiplier=1), reads=[c_idf], writes=[c_idf]))
        setup.append(op(POOL, lambda e: e.memset(ones_bf[:], 1.0)))
        setup.append(op(POOL, lambda e: e.memset(trif, 1.0), writes=[c_tri]))
        for j in range(4):
            setup.append(op(POOL, lambda e, j=j: e.affine_select(
                trif[:, j, :], trif[:, j, :], pattern=[[1, 128]], compare_op=ALU.is_ge,
                fill=0.0, base=0, channel_multiplier=-1), reads=[c_tri], writes=[c_tri]))
        setup.append(op(POOL, lambda e: e.memset(msk_s[:], 0.0), writes=[c_msk]))
        setup.append(op(POOL, lambda e: e.memset(smask[:], 1.0), writes=[c_sm]))
        setup.append(op(POOL, lambda e: e.memset(
            smask[:, 0:TP].rearrange("p (c l) -> p c l", l=128)[:, :, 0:1], 0.0), writes=[c_sm]))
        setup.append(op(POOL, lambda e: e.memset(
            smask[:, TP:TC].rearrange("p (c l) -> p c l", l=32)[:, :, 0:1], 0.0), writes=[c_sm]))
        setup.append(op(POOL, lambda e: e.memset(eps_t[:], EPS)))
        c_rm = Res("rm")
        setup.append(op(POOL, lambda e: e.memset(rowmask[:], 0.0), writes=[c_rm]))
        setup.append(op(POOL, lambda e: e.memset(rowmask[0:32, 0:1], 1.0), writes=[c_rm]))
        setup.append(op(POOL, lambda e: e.memset(rowmask[32:64, 1:2], 1.0), writes=[c_rm]))
        setup.append(op(POOL, lambda e: e.memset(Ucarry[:], 0.0)))
        setup.append(op(POOL, lambda e: e.memset(ebLcarry[:], 1.0)))
        setup.append(op(POOL, lambda e: e.memset(zcarry[:], 0.0)))
        setup.append(op(POOL, lambda e: e.memset(ss[:], 0.0)))
        setup.append(op(SP, lambda e: e.dma_start(
            out=lbraw[:], in_=lb_hgrn.rearrange("r (hc p) -> p r hc", p=128),
            allow_slow_non_contiguous=True), writes=[c_lbraw], dsem=misc_sem))
        setup.append(op(SP, lambda e: e.dma_start(
            out=gnorm[:], in_=g_norm_hgrn[0, :].rearrange("(hc p) -> p hc", p=128),
            allow_slow_non_contiguous=True), dsem=misc_sem))
        setup.append(op(SP, lambda e: e.dma_start(
            out=cwt[:], in_=conv_w.rearrange("j (fc p) -> p j fc", p=128),
            allow_slow_non_contiguous=True), dsem=misc_sem))
        for q in range(2):
            setup.append(op(SP, lambda e, q=q: e.dma_start(
                out=shist[:, q, :, :], in_=s_cv[q].rearrange("r (fc p) -> p r fc", p=128),
                allow_slow_non_contiguous=True), dsem=misc_sem))
        early_loads()
        setup = [ev for ev in setup if ev[0] is not misc_sem] + [(misc_sem, misc_sem.v)]
        for q in queues:
            _wait(q, setup)
        setup2 = []
        setup2.append(op(DVE, lambda e: e.tensor_copy(ident[:], identf)))
        setup2.append(op(DVE, lambda e: e.tensor_copy(tri4[:, :, :], trif), writes=[c_tri]))
        setup2.append(op(DVE, lambda e: e.tensor_copy(msk_s[0:32, 0:32], trif[0:32, 0, 0:32]), writes=[c_msk]))
        setup2.append(op(DVE, lambda e: e.tensor_copy(msk_s[32:64, 32:64], trif[32:64, 0, 32:64]), writes=[c_msk]))
        setup2.append(op(DVE, lambda e: e.tensor_sub(lbm1[:], lbraw[:, 1, :], lbraw[:, 0, :]), reads=[c_lbraw], writes=[c_lb]))
        setup2.append(op(ACT, lambda e: e.activation(out=lbm1[:], in_=lbm1[:], func=AF.Exp), reads=[c_lb], writes=[c_lb]))
        setup2.append(op(ACT, lambda e: e.activation(out=lbm1[:], in_=lbm1[:], func=AF.Ln, bias=1.0), reads=[c_lb], writes=[c_lb]))
        setup2.append(op(ACT, lambda e: e.activation(out=oml[:], in_=lbm1[:], func=AF.Exp, scale=-1.0), reads=[c_lb], writes=[c_oml]))
        setup2.append(op(DVE, lambda e: e.tensor_scalar_add(lbm1[:], oml[:], -1.0), reads=[c_oml], writes=[c_lb]))
        setup2.append(op(DVE, lambda e: e.tensor_scalar(oml[:], oml[:], -1.0, 1.0, ALU.mult, ALU.add), reads=[c_lb, c_oml], writes=[c_oml]))
        for q in queues:
            _wait(q, setup2)

        plan = []
        for t in range(NT):
            for hd in range(8):
                plan.append(("hg", hd))
            for oh in range(2):
                plan.append(("wo", 0, oh))
            for g in range(4):
                plan += [("up", 0, g, 0), ("up", 0, g, 1), ("dn", 0, g, 0), ("dn", 0, g, 1)]
            for fc in range(8):
                plan.append(("cv", fc))
            for oh in range(2):
                plan.append(("wo", 1, oh))
            for g in range(4):
                plan += [("up", 1, g, 0), ("up", 1, g, 1), ("dn", 1, g, 0), ("dn", 1, g, 1)]
        wstate = {"issued": 0, "next": 0}

        def w_src_dst(tag, slot):
            k = tag[0]
            if k == "hg":
                sv = w_in_hgrn.rearrange("(kc p) (t hh j) -> p kc t hh j", p=128, t=4, hh=8)
                dv = slot[:, 0:4096].rearrange("p (kc t j) -> p kc t j", kc=8, t=4)
                return [(sv[:, :, ty, tag[1], :], dv[:, :, ty, :]) for ty in range(4)]
            elif k == "cv":
                sv = w_in_conv.rearrange("(kc p) (t ff j) -> p kc t ff j", p=128, t=3, ff=8)
                dv = slot[:, 0:3072].rearrange("p (kc t j) -> p kc t j", kc=8, t=3)
                return [(sv[:, :, ty, tag[1], :], dv[:, :, ty, :]) for ty in range(3)]
            elif k == "wo":
                wsrc = w_out_hgrn if tag[1] == 0 else w_out_conv
                src = wsrc.rearrange("(kc p) n -> p kc n", p=128)[:, :, tag[2] * 512:(tag[2] + 1) * 512]
            elif k == "up":
                _, l, g, j = tag
                c0 = g * 1024 + j * 512
                src = w_up[l].rearrange("(kc p) n -> p kc n", p=128)[:, :, c0:c0 + 512]
            else:
                _, l, g, oh = tag
                src = w_down[l][g * 1024:(g + 1) * 1024, :].rearrange("(kc p) n -> p kc n", p=128)[:, :, oh * 512:(oh + 1) * 512]
            dst = slot[:, 0:4096].rearrange("p (kc n) -> p kc n", kc=8)
            return [(src, dst)]

        def w_issue_upto(n):
            while wstate["issued"] <= min(n, len(plan) - 1):
                m = wstate["issued"]
                s = m % NSLOT
                pairs = w_src_dst(plan[m], wring[s])
                _wait(POOL, _deps([], [slot_res[s]]))
                for (src, dst) in pairs:
                    ins = POOL.eng.dma_start(out=dst, in_=src)
                    slot_sem[s].v += 16
                    ins.then_inc(slot_sem[s].h, 16)
                _stamp(POOL, (slot_sem[s], slot_sem[s].v))
                _record((slot_sem[s], slot_sem[s].v), [], [slot_res[s]])
                wstate["issued"] += 1

        def w_acquire(tag, ahead=NSLOT - 1):
            n = wstate["next"]
            assert plan[n] == tag, (plan[n], tag)
            wstate["next"] += 1
            w_issue_upto(n + ahead)
            s = n % NSLOT
            return wring[s], slot_res[s]

        w_issue_upto(NSLOT - 2)

        def tile_geom(t):
            blocks = [(b, b * 128, 128) for b in range(8)]
            cts = [(0, 512, [0, 1, 2, 3]), (512, 512, [4, 5, 6, 7])]
            if t == 0:
                blocks.append((8, TP, 64))
                cts.append((TP, 64, [8]))
            return blocks, cts

        def load_x(t, blocks):
            for (b, c0, ntok) in blocks:
                if b < 8:
                    src = xp[t * TP + b * 128: t * TP + (b + 1) * 128, :]
                else:
                    src = xs[:, :]
                op(SP, lambda e, b=b, ntok=ntok, src=src: e.dma_start(out=h[0:ntok, b, :], in_=src),
                   writes=[h_res[b]], dsem=h_sem[b])

        gam_ctr = [0]

        ss_r2 = [[Res("ss%d_%d" % (i, b)) for b in range(9)] for i in range(2)]
        rstd_r2 = [[Res("rstd%d_%d" % (i, b)) for b in range(9)] for i in range(2)]

        def norm_begin(gamma_row, sset=0):
            gi = gam_ctr[0] % 2
            gam_ctr[0] += 1
            op(SP, lambda e: e.dma_start(out=gam[gi][:], in_=gamma_row.broadcast_to([128, D])),
               writes=[gam_res[gi]], dsem=gam_sem[gi])
            op(POOL, lambda e: e.memset(ss2[sset][:], 0.0), writes=ss_r2[sset])
            return gi

        def norm_stats_block(b, ntok, ai=0, sset=0, junk=None):
            sst, rst = ss2[sset], rstd2[sset]
            jt, jr = junk if junk is not None else (aTM[ai], aTM_res[ai])
            op(ACT, lambda e: e.activation(
                out=jt[0:ntok, :], in_=h[0:ntok, b, :], func=AF.Square, accum_out=sst[0:ntok, b:b + 1]),
               reads=[h_res[b]], writes=[jr, ss_r2[sset][b]])
            op(ACT, lambda e: e.activation(out=rst[0:ntok, b:b + 1], in_=sst[0:ntok, b:b + 1], func=AF.Ln,
                                           scale=1.0 / D, bias=eps_t[0:ntok, :]),
               reads=[ss_r2[sset][b]], writes=[rstd_r2[sset][b]])
            op(ACT, lambda e: e.activation(out=rst[0:ntok, b:b + 1], in_=rst[0:ntok, b:b + 1], func=AF.Exp, scale=-0.5),
               reads=[rstd_r2[sset][b]], writes=[rstd_r2[sset][b]])

        atm_ctr = [0]

        def norm_p1(b, c0, ntok, gi, sset=0):
            ai = atm_ctr[0] % NATM
            atm_ctr[0] += 1
            norm_stats_block(b, ntok, ai, sset)
            op(DVE, lambda e: e.scalar_tensor_tensor(
                aTM[ai][0:ntok, :], h[0:ntok, b, :], rstd2[sset][0:ntok, b:b + 1], gam[gi][0:ntok, :],
                ALU.mult, ALU.mult),
               reads=[h_res[b], rstd_r2[sset][b], gam_res[gi]], writes=[aTM_res[ai]])
            return ai

        def norm_p2(b, c0, ntok, ai):
            tb, tr = tp_next()
            pe_group([lambda e, kc=kc: e.transpose(
                tb[:, kc, 0:ntok], aTM[ai][0:ntok, kc * 128:(kc + 1) * 128], ident[0:ntok, 0:ntok])
                for kc in range(KC)], reads=[aTM_res[ai]], writes=[tr])
            op(ACT, lambda e: e.copy(aFM[:, :, c0:c0 + ntok], tb[:, :, 0:ntok]),
               reads=[tr], writes=[a_res[b]])

        def norm_to_fm(blocks, gamma_row, gi=None):
            if gi is None:
                gi = norm_begin(gamma_row)
            else:
                op(POOL, lambda e: e.memset(ss2[0][:], 0.0), writes=ss_r2[0])
            prev = None
            for (b, c0, ntok) in blocks:
                ai = norm_p1(b, c0, ntok, gi)
                if prev is not None:
                    norm_p2(*prev)
                prev = (b, c0, ntok, ai)
            norm_p2(*prev)

        def tm_proj_add(blocks, act_res_of_blk, wtag_fn, stages=None, lags=None, begin_final=None):
            def one(slot_wv, sres, oh, b, c0, ntok):
                bank, bres = mm_next()
                pe_group([lambda e, kc=kc: e.matmul(
                    bank[0:ntok, :], hid[:, kc, c0:c0 + ntok], slot_wv[:, kc, :],
                    start=(kc == 0), stop=(kc == KC - 1)) for kc in range(KC)],
                    reads=[sres, act_res_of_blk(b)], writes=[bres])
                op(DVE, lambda e: e.tensor_tensor(
                    h[0:ntok, b, oh * 512:(oh + 1) * 512], h[0:ntok, b, oh * 512:(oh + 1) * 512],
                    bank[0:ntok, :], ALU.add),
                   reads=[bres, h_res[b]], writes=[h_res[b]])
            if stages is None:
                for oh in range(2):
                    slot, sres = w_acquire(wtag_fn(oh))
                    wv = slot[:, 0:4096].rearrange("p (kc n) -> p kc n", kc=8)
                    for (b, c0, ntok) in blocks:
                        one(wv, sres, oh, b, c0, ntok)
                return
            ws = []
            for oh in range(2):
                slot, sres = w_acquire(wtag_fn(oh), ahead=NSLOT - 1 - oh)
                ws.append((slot[:, 0:4096].rearrange("p (kc n) -> p kc n", kc=8), sres))
            if begin_final is not None:
                begin_final()
            K = len(stages)
            queues_ = [[] for _ in range(K)]

            def step(final=False):
                for k in reversed(range(K)):
                    if queues_[k] and (final or len(queues_[k]) > lags[k]):
                        r = stages[k](*queues_[k].pop(0))
                        if k + 1 < K and r is not None:
                            queues_[k + 1].append(r)
            for (b, c0, ntok) in blocks:
                for oh in range(2):
                    one(ws[oh][0], ws[oh][1], oh, b, c0, ntok)
                queues_[0].append((b, c0, ntok))
                step()
            while any(queues_):
                step(final=True)

        rl_ctr = [0]

        def mlp(l, blocks, cts, stages=None, lags=None, begin_final=None):
            ct_of_blk = {}
            for ci, (c0, cw, bl) in enumerate(cts):
                for b in bl:
                    ct_of_blk[b] = ci
            for g in range(4):
                for j in range(2):
                    slot, sres = w_acquire(("up", l, g, j))
                    wv = slot[:, 0:4096].rearrange("p (kc n) -> p kc n", kc=8)
                    for ocl in range(4):
                        oc = j * 4 + ocl
                        for ci, (c0, cw, bl) in enumerate(cts):
                            bank, bres = mm_next()
                            pe_group([lambda e, kc=kc, c0=c0, cw=cw, bank=bank, wv=wv, ocl=ocl: e.matmul(
                                bank[:, 0:cw], wv[:, kc, ocl * 128:(ocl + 1) * 128], aFM[:, kc, c0:c0 + cw],
                                start=(kc == 0), stop=(kc == KC - 1)) for kc in range(KC)],
                                reads=[sres] + [a_res[b] for b in bl], writes=[bres])
                            ri = rl_ctr[0] % 2
                            rl_ctr[0] += 1
                            op(ACT, lambda e, cw=cw, bank=bank, ri=ri: e.activation(
                                out=rl[ri][:, 0:cw], in_=bank[:, 0:cw], func=AF.Relu),
                               reads=[bres], writes=[rl_res[ri]])
                            op(DVE, lambda e, cw=cw, c0=c0, oc=oc, ri=ri: e.tensor_tensor(
                                hid[:, oc, c0:c0 + cw], rl[ri][:, 0:cw], rl[ri][:, 0:cw], ALU.mult),
                               reads=[rl_res[ri]], writes=[hid_res[ci]])
                tm_proj_add(blocks, lambda b: hid_res[ct_of_blk[b]], lambda oh: ("dn", l, g, oh),
                            stages=(stages if g == 3 else None), lags=lags, begin_final=(begin_final if g == 3 else None))

        fence = []
        yo_ctr = [0]
        for t in range(NT):
            blocks, cts = tile_geom(t)
            ct_of_blk = {}
            for ci, (c0, cw, bl) in enumerate(cts):
                for b in bl:
                    ct_of_blk[b] = ci
            nb = len(blocks)
            has_s = (t == 0)

            nstate = {}

            def mk_hooks(gamma_row):
                def begin():
                    nstate["gi"] = norm_begin(gamma_row)

                def p1(b, c0, ntok):
                    return (b, c0, ntok, norm_p1(b, c0, ntok, nstate["gi"]))

                def p2(b, c0, ntok, ai):
                    norm_p2(b, c0, ntok, ai)
                return [p1, p2], begin
            if t == 0:
                gam_ctr[0] = 1
                norm_to_fm(blocks, norm_mix[0:1, :], gi=early["gi"])
            with ExitStack() as hs:
                def hsb(name, shape, dt):
                    return sb(name + "_t%d" % t, shape, dt, stack=hs)
                newres = []

                def R(name):
                    r = Res(name, pre=fence)
                    newres.append(r)
                    return r
                qt = [hsb("qt%d" % i, [128, TC], BF16) for i in range(2)]
                kt = [hsb("kt%d" % i, [128, TC], BF16) for i in range(2)]
                sgg = [hsb("sgg%d" % i, [128, TC], F32) for i in range(2)]
                vT = [hsb("vT%d" % i, [128, 9, 128], BF16) for i in range(2)]
                ebL = [hsb("ebL%d" % i, [128, 12], F32) for i in range(2)]
                gt = [[hsb("g%s%d" % (n, i), [128, 512], F32) for n in "abcd"] for i in range(2)]
                ktT = hsb("ktT", [128, 9, 128], BF16)
                ATm = hsb("ATm", [128, 9, 128], BF16)
                Sbf = hsb("Sbf", [128, 8, 128], BF16)
                Ubuf = hsb("Ubuf", [128, 9, 128], F32)
                osq = [hsb("osq%d" % i, [128, 512], BF16) for i in range(2)]
                orr = [hsb("orr%d" % i, [128, 512], F32) for i in range(2)]
                qt_r = [[R("qt") for _ in range(3)] for i in range(2)]
                kt_r = [[R("kt") for _ in range(3)] for i in range(2)]
                sgg_r = [[R("sgg") for _ in range(3)] for i in range(2)]
                vT_r = [[R("vT") for _ in range(3)] for i in range(2)]
                ebL_r = [R("ebL") for i in range(2)]
                gt_r = [[R("g") for n in "abcd"] for i in range(2)]
                ktT_r = [R("ktT") for _ in range(3)]
                ATm_r = [R("ATm") for _ in range(3)]
                Sbf_r = [R("Sbf") for _ in range(8)]
                U_r = [R("U") for _ in range(9)]
                osq_r = [R("osq") for i in range(2)]
                orr_r = [R("orr") for i in range(2)]
                if has_s:
                    S0 = hsb("S0", [128, 2, 8, 128], F32)
                    S0bf = hsb("S0bf", [128, 2, 8, 128], BF16)
                    Us = hsb("Us", [128, 2, 128], F32)
                    ktTs = hsb("ktTs", [128, 2, 128], BF16)
                    S0_r = [[R("S0") for _ in range(8)] for q in range(2)]
                    S0bf_r = R("S0bf")
                    Us_r = R("Us")
                    for q in range(2):
                        op(SP, lambda e, q=q: e.dma_start(out=S0[:, q, :, :], in_=s_hg[q].rearrange("h k v -> k h v")),
                           writes=S0_r[q], dsem=s0_sem[q])
                    op(ACT, lambda e: e.copy(S0bf[:].rearrange("p q h v -> p (q h v)"),
                                             S0[:].rearrange("p q h v -> p (q h v)")),
                       reads=S0_r[0] + S0_r[1], writes=[S0bf_r])
                gctr = [0]

                def H1(hd):
                    S = hd % 2
                    st = {}

                    def fm(typ, c0, cw, bl, hold=False):
                        bank, bres = mm_next(hold)
                        wv, sres = st["wv"], st["sres"]
                        pe_group([lambda e, kc=kc: e.matmul(
                            bank[:, 0:cw], wv[:, kc, typ, :], aFM[:, kc, c0:c0 + cw],
                            start=(kc == 0), stop=(kc == KC - 1)) for kc in range(KC)],
                            reads=[sres] + [a_res[b] for b in bl], writes=[bres])
                        return bank, bres

                    def mkA(ci, c0, cw, bl):
                        def A_pe():
                            if "wv" not in st:
                                slot, sres = w_acquire(("hg", hd))
                                st["wv"] = slot[:, 0:4096].rearrange("p (kc t j) -> p kc t j", kc=8, t=4)
                                st["sres"] = sres
                            gi = gctr[0] % 2
                            gctr[0] += 1
                            st[ci] = gi
                            st[("f", ci)] = fm(1, c0, cw, bl, hold=True)

                        def A_act():
                            gi = st[ci]
                            tA, tS, tL, tB = [x[:, 0:cw] for x in gt[gi]]
                            rA, rS, rL, rB = gt_r[gi]
                            bf, bf_r = st[("f", ci)]
                            op(ACT, lambda e: e.activation(out=tA, in_=bf[:, 0:cw], func=AF.Exp), reads=[bf_r], writes=[rA])
                            mm_release(bf_r)
                            op(ACT, lambda e: e.activation(out=tA, in_=tA, func=AF.Ln, bias=1.0), reads=[rA], writes=[rA])
                            op(ACT, lambda e: e.activation(out=tS, in_=tA, func=AF.Exp, scale=-1.0), reads=[rA], writes=[rS])
                            op(ACT, lambda e: e.activation(out=tL, in_=tS, func=AF.Ln, scale=lbm1[:, hd:hd + 1], bias=1.0),
                               reads=[rS], writes=[rL])

                        def A():
                            A_pe()
                            A_act()
                        A.pe = A_pe
                        A.act = A_act
                        return A

                    def mkB(ci, c0, cw, bl):
                        def B():
                            gi = st[ci]
                            wv, sres = st["wv"], st["sres"]
                            tA, tS, tL, tB = [x[:, 0:cw] for x in gt[gi]]
                            rA, rS, rL, rB = gt_r[gi]
                            bg, bg_r = fm(3, c0, cw, bl, hold=True)
                            st[("g", ci)] = (bg, bg_r)
                            bank, bres = mm_next()
                            mms = []
                            for j, b in enumerate(bl):
                                ntok = blocks[b][2]
                                bc0 = blocks[b][1]
                                for kc in range(KC):
                                    mms.append(lambda e, kc=kc, j=j, ntok=ntok, bc0=bc0: e.matmul(
                                        bank[0:ntok, j * 128:(j + 1) * 128], aFM[:, kc, bc0:bc0 + ntok], wv[:, kc, 2, :],
                                        start=(kc == 0), stop=(kc == KC - 1)))
                            pe_group(mms, reads=[sres] + [a_res[b] for b in bl], writes=[bres])
                            if cw == 512:
                                op(ACT, lambda e: e.copy(vT[S][:, bl[0]:bl[0] + 4, :],
                                                         bank[:, :].rearrange("p (j v) -> p j v", j=4)),
                                   reads=[bres], writes=[vT_r[S][ci]])
                            else:
                                op(ACT, lambda e: e.copy(vT[S][0:64, 8, :], bank[0:64, 0:128]),
                                   reads=[bres], writes=[vT_r[S][ci]])
                            bq, bq_r = fm(0, c0, cw, bl)
                            op(DVE, lambda e: e.tensor_tensor_scan(tB, smask[:, c0:c0 + cw], tL, 0.0, ALU.mult, ALU.add),
                               reads=[rL], writes=[rB])
                            op(ACT, lambda e: e.activation(out=tA, in_=tB, func=AF.Exp), reads=[rB], writes=[rA])
                            op(ACT, lambda e: e.activation(out=tL, in_=tB, func=AF.Exp, scale=-1.0), reads=[rB], writes=[rL])
                            op(DVE, lambda e: e.tensor_tensor(qt[S][:, c0:c0 + cw], bq[:, 0:cw], tA, ALU.mult),
                               reads=[bq_r, rA], writes=[qt_r[S][ci]])
                            op(DVE, lambda e: e.scalar_tensor_tensor(kt[S][:, c0:c0 + cw], tS, oml[:, hd:hd + 1], tL,
                                                                     ALU.mult, ALU.mult),
                               reads=[rS, rL], writes=[kt_r[S][ci]])
                            if cw == 512:
                                op(ACT, lambda e: e.copy(ebL[S][:, 1 + 4 * ci:5 + 4 * ci],
                                                         tA.rearrange("p (c l) -> p c l", l=128)[:, :, 127]),
                                   reads=[rA], writes=[ebL_r[S]])
                            else:
                                op(ACT, lambda e: e.copy(ebL[S][:, 9:11],
                                                         tA.rearrange("p (c l) -> p c l", l=32)[:, :, 31]),
                                   reads=[rA], writes=[ebL_r[S]])
                        return B

                    def mkC(ci, c0, cw, bl):
                        def C():
                            gi = st[ci]
                            tA, tS, tL, tB = [x[:, 0:cw] for x in gt[gi]]
                            rA, rS, rL, rB = gt_r[gi]
                            bg, bg_r = st[("g", ci)]
                            op(ACT, lambda e: e.activation(out=tB, in_=bg[:, 0:cw], func=AF.Exp, scale=-1.0),
                               reads=[bg_r], writes=[rB])
                            op(ACT, lambda e: e.activation(out=tB, in_=tB, func=AF.Ln, bias=1.0), reads=[rB], writes=[rB])
                            op(ACT, lambda e: e.activation(out=tB, in_=tB, func=AF.Exp, scale=-1.0), reads=[rB], writes=[rB])
                            op(DVE, lambda e: e.tensor_tensor(sgg[S][:, c0:c0 + cw], bg[:, 0:cw], tB, ALU.mult),
                               reads=[bg_r, rB], writes=[sgg_r[S][ci]])
                            mm_release(bg_r)
                        return C
                    return [(mkA(ci, c0, cw, bl), mkB(ci, c0, cw, bl), mkC(ci, c0, cw, bl))
                            for ci, (c0, cw, bl) in enumerate(cts)]

                def H2(hd):
                    S = hd % 2
                    st = {}

                    def a():
                        for ci, (c0, cw, bl) in enumerate(cts):
                            ab, ab_r = mm_next(hold=True)
                            st[("A", ci)] = (ab, ab_r)
                            if cw == 512:
                                abv = v4(ab)
                                pe_group([lambda e, j=j: e.matmul(
                                    abv[:, j, :], kt[S][:, c0 + j * 128:c0 + (j + 1) * 128],
                                    qt[S][:, c0 + j * 128:c0 + (j + 1) * 128], start=True, stop=True) for j in range(4)],
                                    reads=[kt_r[S][ci], qt_r[S][ci]], writes=[ab_r])
                            else:
                                pe_group([lambda e: e.matmul(ab[0:64, 0:64], kt[S][:, c0:c0 + 64], qt[S][:, c0:c0 + 64],
                                                             start=True, stop=True)],
                                         reads=[kt_r[S][ci], qt_r[S][ci]], writes=[ab_r])
                        tb, tr = tp_next(hold=True)
                        st["T"] = (tb, tr)
                        pe_group([lambda e, j=j: e.transpose(
                            tb[:, j, :], kt[S][:, j * 128:(j + 1) * 128], ident[:, :]) for j in range(8)],
                            reads=[kt_r[S][0], kt_r[S][1]], writes=[tr])
                        if has_s:
                            tb2, tr2 = tp_next(hold=True)
                            st["T2"] = (tb2, tr2)
                            pe_group([lambda e: e.transpose(tb2[0:64, 0, :], kt[S][:, TP:TP + 64], ident[:, :])],
                                     reads=[kt_r[S][2]], writes=[tr2])

                    def b():
                        op(DVE, lambda e: e.tensor_copy(Ubuf[:, 0, :], Ucarry[:, hd, :]), reads=[ucarry_res[hd]], writes=[U_r[0]])
                        op(DVE, lambda e: e.tensor_copy(ebL[S][:, 0:1], ebLcarry[:, hd:hd + 1]),
                           reads=[eblc_res[hd]], writes=[ebL_r[S]])
                        for ci, (c0, cw, bl) in enumerate(cts):
                            ab, ab_r = st[("A", ci)]
                            if cw == 512:
                                op(DVE, lambda e: e.tensor_tensor(ATm[:, bl[0]:bl[0] + 4, :], v4(ab), tri4[:, :, :], ALU.mult),
                                   reads=[ab_r], writes=[ATm_r[ci]])
                            else:
                                op(DVE, lambda e: e.tensor_tensor(ATm[0:64, 8, 0:64], ab[0:64, 0:64], msk_s[0:64, 0:64], ALU.mult),
                                   reads=[ab_r], writes=[ATm_r[ci]])
                            mm_release(ab_r)
                        tb, tr = st["T"]
                        op(ACT, lambda e: e.copy(ktT[:, 0:8, :], tb[:, 0:8, :]), reads=[tr], writes=[ktT_r[0], ktT_r[1]])
                        mm_release(tr)
                        if has_s:
                            tb2, tr2 = st["T2"]
                            for q in range(2):
                                op(ACT, lambda e, q=q: e.activation(out=ktTs[0:64, q, :], in_=tb2[0:64, 0, :], func=AF.Identity,
                                                                    scale=rowmask[0:64, q:q + 1]),
                                   reads=[tr2], writes=[ktT_r[2]])
                            mm_release(tr2)

                    def c():
                        for ci, (c0, cw, bl) in enumerate(cts):
                            kb, kb_r = mm_next()
                            kbv = v4(kb)
                            if cw == 512:
                                pe_group([lambda e, j=j: e.matmul(
                                    kbv[:, j, :], ktT[:, bl[0] + j, :], vT[S][:, bl[0] + j, :], start=True, stop=True)
                                    for j in range(4)], reads=[ktT_r[ci], vT_r[S][ci]], writes=[kb_r])
                                for j in range(4):
                                    cix = bl[0] + j
                                    op(DVE, lambda e, cix=cix, j=j: e.scalar_tensor_tensor(
                                        Ubuf[:, cix + 1, :], Ubuf[:, cix, :], ebL[S][:, cix:cix + 1], kbv[:, j, :], ALU.mult, ALU.add),
                                       reads=[U_r[cix], ebL_r[S], kb_r], writes=[U_r[cix + 1]])
                            else:
                                pe_group([lambda e, q=q: e.matmul(
                                    kbv[:, q, :], ktTs[0:64, q, :], vT[S][0:64, 8, :],
                                    start=True, stop=True) for q in range(2)], reads=[ktT_r[ci], vT_r[S][ci]], writes=[kb_r])
                                for q in range(2):
                                    op(DVE, lambda e, q=q: e.tensor_tensor(Us[:, q, :], S0[:, q, hd, :], kbv[:, q, :], ALU.add),
                                       reads=[S0_r[q][hd], kb_r], writes=[Us_r])
                                    op(ACT, lambda e, q=q: e.activation(out=S0[:, q, hd, :], in_=Us[:, q, :], func=AF.Identity,
                                                                        scale=ebL[S][:, 9 + q:10 + q]),
                                       reads=[Us_r, ebL_r[S], S0bf_r], writes=[S0_r[q][hd]])

                    def c2():
                        op(DVE, lambda e: e.tensor_tensor(
                            Sbf[:, 0:8, :], Ubuf[:, 0:8, :],
                            ebL[S][:, 0:8].unsqueeze(2).broadcast_to([128, 8, 128]), ALU.mult),
                           reads=U_r[0:8] + [ebL_r[S]], writes=Sbf_r)

                    def d():
                        op(DVE, lambda e: e.tensor_copy(Ucarry[:, hd, :], Ubuf[:, 8, :]), reads=[U_r[8]], writes=[ucarry_res[hd]])
                        op(DVE, lambda e: e.tensor_copy(ebLcarry[:, hd:hd + 1], ebL[S][:, 8:9]),
                           reads=[ebL_r[S]], writes=[eblc_res[hd]])
                        for ci in range(2):
                            d_ct(ci)

                    def d_ct(ci):
                        if True:
                            c0, cw, bl = cts[ci]
                            bo, bo_r = mm_next(hold=True)
                            st[("o", ci)] = (bo, bo_r)
                            mms = []
                            if cw == 512:
                                for j in range(4):
                                    cix = bl[0] + j
                                    cc0 = c0 + j * 128
                                    mms.append(lambda e, j=j, cix=cix: e.matmul(
                                        bo[:, j * 128:(j + 1) * 128], vT[S][:, cix, :], ATm[:, cix, :], start=True, stop=False))
                                    mms.append(lambda e, j=j, cix=cix, cc0=cc0: e.matmul(
                                        bo[:, j * 128:(j + 1) * 128], Sbf[:, cix, :], qt[S][:, cc0:cc0 + 128], start=False, stop=True))
                                r2 = [Sbf_r[bl[0] + j] for j in range(4)] + [qt_r[S][ci]]
                            else:
                                for q in range(2):
                                    mms.append(lambda e, q=q: e.matmul(
                                        bo[:, 32 * q:32 * q + 32], vT[S][0:64, 8, :], ATm[0:64, 8, 32 * q:32 * q + 32],
                                        start=True, stop=False))
                                    mms.append(lambda e, q=q: e.matmul(
                                        bo[:, 32 * q:32 * q + 32], S0bf[:, q, hd, :], qt[S][:, c0 + 32 * q:c0 + 32 * q + 32],
                                        start=False, stop=True))
                                r2 = [S0bf_r, qt_r[S][ci]]
                            pe_group(mms, reads=[vT_r[S][ci], ATm_r[ci]] + r2, writes=[bo_r])

                    def e_():
                        for ci in range(2):
                            e_ct(ci)

                    def e_ct(ci):
                        if True:
                            c0, cw, bl = cts[ci]
                            bo, bo_r = st[("o", ci)]
                            oi = ci % 2
                            op(ACT, lambda e: e.activation(out=osq[oi][:, 0:cw], in_=bo[:, 0:cw], func=AF.Square),
                               reads=[bo_r], writes=[osq_r[oi]])
                            bs, bs_r = mm_next(hold=True)
                            st[("s", ci)] = (bs, bs_r)
                            pe_group([lambda e: e.matmul(bs[:, 0:cw], ones_bf[:, :], osq[oi][:, 0:cw], start=True, stop=True)],
                                     reads=[osq_r[oi]], writes=[bs_r])

                    def f_():
                        for ci in range(2):
                            f_ct(ci)
                        if has_s:
                            d_ct(2)
                            e_ct(2)
                            f_ct(2)

                    def f_ct(ci):
                        if True:
                            c0, cw, bl = cts[ci]
                            bo, bo_r = st[("o", ci)]
                            bs, bs_r = st[("s", ci)]
                            oi = ci % 2
                            op(ACT, lambda e: e.activation(out=orr[oi][:, 0:cw], in_=bs[:, 0:cw], func=AF.Ln,
                                                           scale=1.0 / 128, bias=eps_t[:]),
                               reads=[bs_r], writes=[orr_r[oi]])
                            op(ACT, lambda e: e.activation(out=orr[oi][:, 0:cw], in_=orr[oi][:, 0:cw], func=AF.Exp, scale=-0.5),
                               reads=[orr_r[oi]], writes=[orr_r[oi]])
                            op(DVE, lambda e: e.tensor_tensor(orr[oi][:, 0:cw], orr[oi][:, 0:cw], sgg[S][:, c0:c0 + cw], ALU.mult),
                               reads=[orr_r[oi], sgg_r[S][ci]], writes=[orr_r[oi]])
                            op(DVE, lambda e: e.scalar_tensor_tensor(hid[:, hd, c0:c0 + cw], bo[:, 0:cw], gnorm[:, hd:hd + 1],
                                                                     orr[oi][:, 0:cw], ALU.mult, ALU.mult),
                               reads=[bo_r, orr_r[oi]], writes=[hid_res[ci]])
                            mm_release(bo_r)
                            mm_release(bs_r)
                    def cc_():
                        c()
                        c2()
                    return [a, b, cc_, d, e_, f_]

                h1s = {hd: H1(hd) for hd in range(8)}
                h2s = {hd: H2(hd) for hd in range(8)}

                def n_(P, ci, k, part=None):
                    hd = P + 1
                    if 0 <= hd < 8 and ci < len(h1s[hd]):
                        fn = h1s[hd][ci][k]
                        if part is not None:
                            fn = getattr(fn, part)
                        fn()

                def p_(P, k):
                    if 0 <= P < 8:
                        h2s[P][k]()
                for P in range(-1, 9):
                    p_(P - 1, 4)
                    n_(P, 0, 0, "pe")
                    p_(P - 1, 5)
                    n_(P, 0, 0, "act")
                    n_(P, 0, 1)
                    p_(P, 0)
                    n_(P, 0, 2)
                    p_(P, 1)
                    p_(P, 2)
                    n_(P, 1, 0)
                    n_(P, 1, 1)
                    p_(P, 3)
                    n_(P, 1, 2)
                    for k in range(3):
                        n_(P, 2, k)
                if has_s:
                    for q in range(2):
                        op(SP, lambda e, q=q: e.dma_start(out=hgs[q].rearrange("h k v -> k h v"), in_=S0[:, q, :, :]),
                           reads=S0_r[q], dsem=out_sem)
                if t == NT - 1:
                    fin = sgg[0][:, 0:1024].rearrange("p (h v) -> p h v", h=8)
                    for hd in range(8):
                        op(ACT, lambda e, hd=hd: e.activation(out=fin[:, hd, :], in_=Ucarry[:, hd, :], func=AF.Identity,
                                                              scale=ebLcarry[:, hd:hd + 1]),
                           reads=[ucarry_res[hd], eblc_res[hd]] + sgg_r[0], writes=sgg_r[0])
                    op(SP, lambda e: e.dma_start(out=hgp.rearrange("h k v -> k h v"), in_=fin), reads=sgg_r[0], dsem=out_sem)
                fin, beg = mk_hooks(norm_mlp[0:1, :])
                tm_proj_add(blocks, lambda b: hid_res[ct_of_blk[b]], lambda oh: ("wo", 0, oh), stages=fin, lags=[1, 1], begin_final=beg)
                fence = []
                for r in newres:
                    fence.extend(r.w)
                    fence.extend(r.r.values())
                best = {}
                for (sm, v) in fence:
                    if id(sm) not in best or best[id(sm)][1] < v:
                        best[id(sm)] = (sm, v)
                fence = list(best.values())
            fin, beg = mk_hooks(norm_mix[1:2, :])
            mlp(0, blocks, cts, stages=fin, lags=[1, 1], begin_final=beg)

            with ExitStack() as cs:
                def csb(name, shape, dt):
                    return sb(name + "_t%d" % t, shape, dt, stack=cs)
                newres = []

                def R(name):
                    r = Res(name, pre=fence)
                    newres.append(r)
                    return r
                ZW = TP + 2 + 68
                zb = [csb("zb%d" % i, [128, ZW], F32) for i in range(2)]
                cc = [csb("cc%d" % i, [128, ZW], F32) for i in range(2)]
                cgs = [csb("cgs%d" % i, [128, 512], F32) for i in range(2)]
                bgs = [csb("bgs%d" % i, [128, TC], F32) for i in range(2)]
                zb_r = [R("zb") for i in range(2)]
                cc_r = [R("cc") for i in range(2)]
                cgs_r = [R("cgs") for i in range(2)]
                bgs_r = [R("bgs") for i in range(2)]
                if has_s:
                    scv = csb("scv", [128, 2, 2, 8], F32)
                    scv_r = R("scv")
                cg_ctr = [0]
                L = (TP + 68) if has_s else TP
                for fc in range(8):
                    Z = fc % 2
                    slot, sres = w_acquire(("cv", fc))
                    wv = slot[:, 0:3072].rearrange("p (kc t j) -> p kc t j", kc=8, t=3)
                    op(DVE, lambda e: e.tensor_copy(zb[Z][:, 0:2], zcarry[:, :, fc]), reads=[zcarry_res[fc]], writes=[zb_r[Z]])
                    if has_s:
                        op(DVE, lambda e: e.tensor_copy(
                            zb[Z][:, TP + 2:TP + 70].rearrange("p (q l) -> p q l", q=2)[:, :, 0:2], shist[:, :, :, fc]),
                           reads=[], writes=[zb_r[Z]])

                    def fm(typ, c0, cw, bl):
                        bank, bres = mm_next()
                        pe_group([lambda e, kc=kc: e.matmul(
                            bank[:, 0:cw], wv[:, kc, typ, :], aFM[:, kc, c0:c0 + cw],
                            start=(kc == 0), stop=(kc == KC - 1)) for kc in range(KC)],
                            reads=[sres] + [a_res[b] for b in bl], writes=[bres])
                        return bank, bres
                    for ci, (c0, cw, bl) in enumerate(cts):
                        bc, bc_r = fm(1, c0, cw, bl)
                        bu, bu_r = fm(2, c0, cw, bl)
                        bb, bb_r = fm(0, c0, cw, bl)
                        gi = cg_ctr[0] % 2
                        cg_ctr[0] += 1
                        op(ACT, lambda e: e.copy(cgs[gi][:, 0:cw], bc[:, 0:cw]), reads=[bc_r], writes=[cgs_r[gi]])
                        if cw == 512:
                            op(DVE, lambda e: e.tensor_tensor(zb[Z][:, 2 + c0:2 + c0 + cw], bu[:, 0:cw], cgs[gi][:, 0:cw], ALU.mult),
                               reads=[bu_r, cgs_r[gi]], writes=[zb_r[Z]])
                        else:
                            op(DVE, lambda e: e.tensor_tensor(
                                zb[Z][:, TP + 2:TP + 70].rearrange("p (q l) -> p q l", q=2)[:, :, 2:34],
                                bu[:, 0:64].rearrange("p (q l) -> p q l", q=2),
                                cgs[gi][:, 0:64].rearrange("p (q l) -> p q l", q=2), ALU.mult),
                               reads=[bu_r, cgs_r[gi]], writes=[zb_r[Z]])
                        op(ACT, lambda e: e.copy(bgs[Z][:, c0:c0 + cw], bb[:, 0:cw]), reads=[bb_r], writes=[bgs_r[Z]])
                    op(DVE, lambda e: e.tensor_scalar_mul(cc[Z][:, 0:L], zb[Z][:, 0:L], cwt[:, 0, fc:fc + 1]),
                       reads=[zb_r[Z]], writes=[cc_r[Z]])
                    op(DVE, lambda e: e.scalar_tensor_tensor(cc[Z][:, 0:L], zb[Z][:, 1:L + 1], cwt[:, 1, fc:fc + 1],
                                                             cc[Z][:, 0:L], ALU.mult, ALU.add),
                       reads=[zb_r[Z], cc_r[Z]], writes=[cc_r[Z]])
                    op(DVE, lambda e: e.scalar_tensor_tensor(cc[Z][:, 0:L], zb[Z][:, 2:L + 2], cwt[:, 2, fc:fc + 1],
                                                             cc[Z][:, 0:L], ALU.mult, ALU.add),
                       reads=[zb_r[Z], cc_r[Z]], writes=[cc_r[Z]])
                    op(DVE, lambda e: e.tensor_tensor(hid[:, fc, 0:TP], bgs[Z][:, 0:TP], cc[Z][:, 0:TP], ALU.mult),
                       reads=[bgs_r[Z], cc_r[Z]], writes=[hid_res[0], hid_res[1]])
                    if has_s:
                        op(DVE, lambda e: e.tensor_tensor(
                            hid[:, fc, TP:TC].rearrange("p (q l) -> p q l", q=2),
                            bgs[Z][:, TP:TC].rearrange("p (q l) -> p q l", q=2),
                            cc[Z][:, TP + 2:TP + 70].rearrange("p (q l) -> p q l", q=2)[:, :, 0:32], ALU.mult),
                           reads=[bgs_r[Z], cc_r[Z]], writes=[hid_res[2]])
                        op(DVE, lambda e: e.tensor_copy(
                            scv[:, :, :, fc], zb[Z][:, TP + 2:TP + 70].rearrange("p (q l) -> p q l", q=2)[:, :, 32:34]),
                           reads=[zb_r[Z]], writes=[scv_r])
                    op(DVE, lambda e: e.tensor_copy(zcarry[:, :, fc], zb[Z][:, TP:TP + 2]),
                       reads=[zb_r[Z]], writes=[zcarry_res[fc]])
                if has_s:
                    for q in range(2):
                        op(SP, lambda e, q=q: e.dma_start(out=cvs[q].rearrange("r (fc p) -> p r fc", p=128),
                                                          in_=scv[:, q, :, :], allow_slow_non_contiguous=True),
                           reads=[scv_r], dsem=out_sem)
                if t == NT - 1:
                    op(SP, lambda e: e.dma_start(out=cvp.rearrange("r (fc p) -> p r fc", p=128), in_=zcarry[:, :, :],
                                                 allow_slow_non_contiguous=True),
                       reads=zcarry_res, dsem=out_sem)
                fin, beg = mk_hooks(norm_mlp[1:2, :])
                tm_proj_add(blocks, lambda b: hid_res[ct_of_blk[b]], lambda oh: ("wo", 1, oh), stages=fin, lags=[1, 1], begin_final=beg)
                fence = []
                for r in newres:
                    fence.extend(r.w)
                    fence.extend(r.r.values())
                best = {}
                for (sm, v) in fence:
                    if id(sm) not in best or best[id(sm)][1] < v:
                        best[id(sm)] = (sm, v)
                fence = list(best.values())
            nblocks = tile_geom(t + 1)[0] if t + 1 < NT else []
            nb_by_idx = {b: (b, c0, ntok) for (b, c0, ntok) in nblocks}

            def begin_tail():
                nstate["gf"] = norm_begin(norm_final[0:1, :], sset=0)
                if t + 1 < NT:
                    nstate["gn"] = norm_begin(norm_mix[0:1, :], sset=1)

            def fin_tail(b, c0, ntok):
                gi = nstate["gf"]
                yi = yo_ctr[0] % NYO
                yo_ctr[0] += 1
                norm_stats_block(b, ntok, 0, 0, junk=(yo[yi], yo_res[yi]))
                op(DVE, lambda e: e.scalar_tensor_tensor(
                    yo[yi][0:ntok, :], h[0:ntok, b, :], rstd[0:ntok, b:b + 1], gam[gi][0:ntok, :], ALU.mult, ALU.mult),
                   reads=[h_res[b], rstd_r2[0][b], gam_res[gi]], writes=[yo_res[yi]])
                if b < 8:
                    dst = yp[t * TP + b * 128: t * TP + (b + 1) * 128, :]
                else:
                    dst = ys[:, :]
                op(SP, lambda e: e.dma_start(out=dst, in_=yo[yi][0:ntok, :]),
                   reads=[yo_res[yi]], dsem=yo_sem[yi])
                if b in nb_by_idx:
                    load_x(t + 1, [nb_by_idx[b]])
                    return nb_by_idx[b]
                return None

            def next_p1(b, c0, ntok):
                return (b, c0, ntok, norm_p1(b, c0, ntok, nstate["gn"], sset=1))

            def next_p2(b, c0, ntok, ai):
                norm_p2(b, c0, ntok, ai)
            mlp(1, blocks, cts, stages=[fin_tail, next_p1, next_p2], lags=[0, 2, 1], begin_final=begin_tail)

        for sm in yo_sem + [out_sem]:
            if sm.v > 0:
                SP.eng.wait_ge(sm.h, sm.v)
    return nc


_NC_CACHE = {}


def kernel(x_prompt, x_sample, state_hgrn, state_conv, norm_mix, norm_mlp, norm_final,
           w_in_hgrn, lb_hgrn, g_norm_hgrn, w_out_hgrn, w_in_conv, conv_w, w_out_conv,
           w_up, w_down):
    f = lambda a: np.ascontiguousarray(np.asarray(a, dtype=np.float32))
    x_prompt, x_sample, state_hgrn, state_conv = f(x_prompt), f(x_sample), f(state_hgrn), f(state_conv)
    B, SEQ, _ = x_prompt.shape
    NT = SEQ // TP
    n = 8
    if NT not in _NC_CACHE:
        _NC_CACHE[NT] = build(NT)
    nc = _NC_CACHE[NT]
    shared = {
        "norm_mix": f(norm_mix), "norm_mlp": f(norm_mlp), "norm_final": f(norm_final).reshape(1, D),
        "w_in_hgrn": f(w_in_hgrn)[0], "lb_hgrn": f(lb_hgrn), "g_norm_hgrn": f(g_norm_hgrn).reshape(1, D),
        "w_out_hgrn": f(w_out_hgrn)[0], "w_in_conv": f(w_in_conv)[0], "conv_w": f(conv_w)[0],
        "w_out_conv": f(w_out_conv)[0], "w_up": f(w_up), "w_down": f(w_down),
    }
    in_maps = []
    for c in range(n):
        m = dict(shared)
        m["xp"] = x_prompt[c]
        m["xs"] = x_sample[2 * c:2 * c + 2].reshape(64, D)
        m["s_hg"] = state_hgrn[0, 2 * c:2 * c + 2]
        m["s_cv"] = state_conv[0, 2 * c:2 * c + 2]
        in_maps.append(m)
    res = run_bass_kernel_spmd(nc, in_maps, core_ids=list(range(n)))
    rs = res.results
    y_prompt = np.stack([r["yp"] for r in rs], 0)
    y_sample = np.concatenate([r["ys"].reshape(2, 32, D) for r in rs], 0)
    hg_p = np.stack([r["hgp"] for r in rs], 0)[None]
    cv_p = np.stack([r["cvp"] for r in rs], 0)[None]
    hg_s = np.concatenate([r["hgs"] for r in rs], 0)[None]
    cv_s = np.concatenate([r["cvs"] for r in rs], 0)[None]
    return (y_prompt.astype(np.float32), y_sample.astype(np.float32), hg_p.astype(np.float32),
            cv_p.astype(np.float32), hg_s.astype(np.float32), cv_s.astype(np.float32))
```

```python
import numpy as np
from contextlib import ExitStack
import concourse.bass as bass
import concourse.mybir as mybir
from concourse.bass_utils import run_bass_kernel_spmd

F32 = mybir.dt.float32
BF16 = mybir.dt.bfloat16
AF = mybir.ActivationFunctionType
ALU = mybir.AluOpType

D = 1024
KC = 8
TP = 1024
TC = TP + 64
EPS = 1e-6
NSLOT = 4


class Sem:
    def __init__(self, h):
        self.h = h
        self.v = 0


class Res:
    __slots__ = ("w", "r", "name")

    def __init__(self, name="", pre=None):
        self.w = list(pre) if pre else []
        self.r = {}
        self.name = name


class Q:
    def __init__(self, name, eng, sem):
        self.name = name
        self.eng = eng
        self.sem = sem
        self.seen = {}
        self.lag = 1 << 30


def _deps(reads, writes):
    d = []
    for r in reads:
        d.extend(r.w)
    for w in writes:
        d.extend(w.w)
        d.extend(w.r.values())
    return d


CLK = {}


def _merge(seen, clk):
    for k, v in clk.items():
        if seen.get(k, 0) < v:
            seen[k] = v


def _wait(q, deps, fold=False):
    best = {}
    for (sm, v) in deps:
        if sm is q.sem and v <= sm.v - q.lag:
            continue
        if id(sm) not in best or best[id(sm)][1] < v:
            best[id(sm)] = (sm, v)
    need = []
    for (sm, v) in sorted(best.values(), key=lambda e: -len(CLK.get((id(e[0]), e[1]), ()))):
        if q.seen.get(id(sm), 0) >= v:
            continue
        q.seen[id(sm)] = v
        clk = CLK.get((id(sm), v))
        if clk is not None:
            _merge(q.seen, clk)
        need.append((sm, v))
    last = None
    if fold and need:
        last = need.pop()
    for (sm, v) in need:
        q.eng.wait_ge(sm.h, v)
    return last


def _stamp(q, ev):
    clk = dict(q.seen)
    clk[id(q.sem)] = max(clk.get(id(q.sem), 0), q.sem.v if ev[0] is q.sem else clk.get(id(q.sem), 0))
    clk[id(ev[0])] = max(clk.get(id(ev[0]), 0), ev[1])
    CLK[(id(ev[0]), ev[1])] = clk


def _record(ev, reads, writes):
    sm = ev[0]
    for r in reads:
        r.r[id(sm)] = ev
    for w in writes:
        w.w = [ev]
        w.r = {}


def op(q, fn, reads=(), writes=(), dsem=None):
    last = _wait(q, _deps(reads, writes), fold=True)
    ins = fn(q.eng)
    if last is not None:
        ins._wait_ge(last[0].h, last[1])
    if dsem is not None:
        dsem.v += 16
        ins.then_inc(dsem.h, 16)
        ev = (dsem, dsem.v)
    else:
        q.sem.v += 1
        ins.then_inc(q.sem.h, 1)
        ev = (q.sem, q.sem.v)
    _stamp(q, ev)
    _record(ev, reads, writes)
    return ev


def build(NT=4):
    CLK.clear()
    SEQ = NT * TP
    nc = bass.Bass("TRN2", target_bir_lowering=False)
    dr = lambda n, s, k: nc.dram_tensor(n, s, F32, kind=k).ap()
    xp = dr("xp", [SEQ, D], "ExternalInput")
    xs = dr("xs", [64, D], "ExternalInput")
    s_hg = dr("s_hg", [2, 8, 128, 128], "ExternalInput")
    s_cv = dr("s_cv", [2, 2, D], "ExternalInput")
    norm_mix = dr("norm_mix", [2, D], "ExternalInput")
    norm_mlp = dr("norm_mlp", [2, D], "ExternalInput")
    norm_final = dr("norm_final", [1, D], "ExternalInput")
    w_in_hgrn = dr("w_in_hgrn", [D, 4 * D], "ExternalInput")
    lb_hgrn = dr("lb_hgrn", [2, D], "ExternalInput")
    g_norm_hgrn = dr("g_norm_hgrn", [1, D], "ExternalInput")
    w_out_hgrn = dr("w_out_hgrn", [D, D], "ExternalInput")
    w_in_conv = dr("w_in_conv", [D, 3 * D], "ExternalInput")
    conv_w = dr("conv_w", [3, D], "ExternalInput")
    w_out_conv = dr("w_out_conv", [D, D], "ExternalInput")
    w_up = dr("w_up", [2, D, 4 * D], "ExternalInput")
    w_down = dr("w_down", [2, 4 * D, D], "ExternalInput")
    yp = dr("yp", [SEQ, D], "ExternalOutput")
    ys = dr("ys", [64, D], "ExternalOutput")
    hgp = dr("hgp", [8, 128, 128], "ExternalOutput")
    cvp = dr("cvp", [2, D], "ExternalOutput")
    hgs = dr("hgs", [2, 8, 128, 128], "ExternalOutput")
    cvs = dr("cvs", [2, 2, D], "ExternalOutput")

    with ExitStack() as es:
        def sb(name, shape, dt, stack=es):
            return stack.enter_context(nc.sbuf_tensor(name, shape, dt))

        def newsem(name):
            return Sem(es.enter_context(nc.semaphore(name)))

        h = sb("h", [128, 9, D], F32)
        aFM = sb("aFM", [128, KC, TC], BF16)
        hid = sb("hid", [128, KC, TC], BF16)
        wring = [sb("wring%d" % i, [128, 4096], BF16) for i in range(NSLOT)]
        ident = sb("ident", [128, 128], BF16)
        ones_bf = sb("ones_bf", [128, 128], BF16)
        tri4 = sb("tri4", [128, 4, 128], BF16)
        msk_s = sb("msk_s", [128, 64], F32)
        smask = sb("smask", [128, TC], F32)
        eps_t = sb("eps_t", [128, 1], F32)
        lbraw = sb("lbraw", [128, 2, 8], F32)
        lbm1 = sb("lbm1", [128, 8], F32)
        oml = sb("oml", [128, 8], F32)
        gnorm = sb("gnorm", [128, 8], F32)
        cwt = sb("cwt", [128, 3, 8], F32)
        gam = [sb("gam%d" % i, [128, D], F32) for i in range(2)]
        NATM = 2
        aTM = [sb("aTM%d" % i, [128, D], BF16) for i in range(NATM)]
        ss2 = [sb("ss%d" % i, [128, 9], F32) for i in range(2)]
        rstd2 = [sb("rstd%d" % i, [128, 9], F32) for i in range(2)]
        ss = ss2[0]
        rstd = rstd2[0]
        NYO = 3
        yo = [sb("yo%d" % i, [128, D], F32) for i in range(NYO)]
        identf = yo[0][:, 0:128]
        trif = yo[1][:, 0:512].rearrange("p (a b) -> p a b", a=4)
        Ucarry = sb("Ucarry", [128, 8, 128], F32)
        ebLcarry = sb("ebLcarry", [128, 8], F32)
        zcarry = sb("zcarry", [128, 2, 8], F32)
        rl = [sb("rl%d" % i, [128, 512], BF16) for i in range(2)]
        shist = sb("shist", [128, 2, 2, 8], F32)
        rowmask = sb("rowmask", [128, 2], F32)

        def ps(name, shape, dt):
            return es.enter_context(nc.psum_tensor(name, shape, dt))
        NB = 8
        mm_banks = [ps("mm%d" % i, [128, 512], F32) for i in range(NB)]
        mm_res = [Res("mm%d" % i) for i in range(NB)]
        mm_ctr = [0]

        held = set()

        def mm_next(hold=False):
            for _ in range(NB + 1):
                i = mm_ctr[0] % NB
                mm_ctr[0] += 1
                if i not in held:
                    break
            else:
                raise RuntimeError("no free PSUM bank")
            assert i not in held
            if hold:
                held.add(i)
            return mm_banks[i], mm_res[i]

        def mm_release(res):
            held.discard(mm_res.index(res))

        def v4(bank):
            return bank[:, :].rearrange("p (a b) -> p a b", a=4)

        def tp_next(hold=False):
            bank, r = mm_next(hold)
            return bank[:, :].bitcast(BF16).rearrange("p (a b) -> p a b", a=8), r

        PE = Q("pe", nc.tensor, newsem("s_pe"))
        ACT = Q("act", nc.scalar, newsem("s_act"))
        DVE = Q("dve", nc.vector, newsem("s_dve"))
        POOL = Q("pool", nc.gpsimd, newsem("s_pool"))
        SP = Q("sp", nc.sync, newsem("s_sp"))
        POOL.lag = 1 << 30
        PE.lag = 0
        queues = [PE, ACT, DVE, POOL, SP]
        slot_sem = [newsem("s_slot%d" % i) for i in range(NSLOT)]
        h_sem = [newsem("s_h%d" % i) for i in range(9)]
        gam_sem = [newsem("s_gam%d" % i) for i in range(2)]
        yo_sem = [newsem("s_yo%d" % i) for i in range(NYO)]
        misc_sem = newsem("s_misc")
        s0_sem = [newsem("s_s0%d" % i) for i in range(2)]
        out_sem = newsem("s_out")

        es.enter_context(nc.Block())

        def pe_group(mms, reads, writes):
            last = _wait(PE, _deps(reads, writes), fold=True)
            ins = None
            for m in mms:
                ins = m(PE.eng)
                if last is not None:
                    ins._wait_ge(last[0].h, last[1])
                    last = None
            PE.sem.v += 1
            ins.then_inc(PE.sem.h, 1)
            ev = (PE.sem, PE.sem.v)
            _stamp(PE, ev)
            _record(ev, reads, writes)
            return ev

        h_res = [Res("h%d" % b) for b in range(9)]
        a_res = [Res("a%d" % b) for b in range(9)]
        hid_res = [Res("hid%d" % c) for c in range(3)]
        slot_res = [Res("slot%d" % i) for i in range(NSLOT)]
        gam_res = [Res("gam%d" % i) for i in range(2)]
        aTM_res = [Res("aTM%d" % i) for i in range(NATM)]
        yo_res = [Res("yo%d" % i) for i in range(NYO)]
        rl_res = [Res("rl%d" % i) for i in range(2)]
        sqj_res = Res("sqj")
        ss_res = Res("ss")
        rstd_res = Res("rstd")
        ucarry_res = [Res("uc%d" % i) for i in range(8)]
        eblc_res = [Res("ec%d" % i) for i in range(8)]
        zcarry_res = [Res("zc%d" % i) for i in range(8)]

        early = {}

        def early_loads():
            for b in range(9):
                ntok = 128 if b < 8 else 64
                src = xp[b * 128:(b + 1) * 128, :] if b < 8 else xs[:, :]
                op(SP, lambda e, b=b, ntok=ntok, src=src: e.dma_start(out=h[0:ntok, b, :], in_=src),
                   writes=[h_res[b]], dsem=h_sem[b])
            op(SP, lambda e: e.dma_start(out=gam[0][:], in_=norm_mix[0:1, :].broadcast_to([128, D])),
               writes=[gam_res[0]], dsem=gam_sem[0])
            early["gi"] = 0

        setup = []
        c_idf, c_tri, c_msk, c_sm, c_lb, c_oml, c_lbraw = [Res(n) for n in ("idf", "tri", "msk", "sm", "lb", "oml", "lbraw")]
        setup.append(op(POOL, lambda e: e.memset(identf, 0.0), writes=[c_idf]))
        setup.append(op(POOL, lambda e: e.affine_select(
            identf, identf, pattern=[[-1, 128]], compare_op=ALU.not_equal,
            fill=1.0, base=0, channel_multiplier=1), reads=[c_idf], writes=[c_idf]))
        setup.append(op(POOL, lambda e: e.memset(ones_bf[:], 1.0)))
        setup.append(op(POOL, lambda e: e.memset(trif, 1.0), writes=[c_tri]))
        for j in range(4):
            setup.append(op(POOL, lambda e, j=j: e.affine_select(
                trif[:, j, :], trif[:, j, :], pattern=[[1, 128]], compare_op=ALU.is_ge,
                fill=0.0, base=0, channel_multiplier=-1), reads=[c_tri], writes=[c_tri]))
        setup.append(op(POOL, lambda e: e.memset(msk_s[:], 0.0), writes=[c_msk]))
        setup.append(op(POOL, lambda e: e.memset(smask[:], 1.0), writes=[c_sm]))
        setup.append(op(POOL, lambda e: e.memset(
            smask[:, 0:TP].rearrange("p (c l) -> p c l", l=128)[:, :, 0:1], 0.0), writes=[c_sm]))
        setup.append(op(POOL, lambda e: e.memset(
            smask[:, TP:TC].rearrange("p (c l) -> p c l", l=32)[:, :, 0:1], 0.0), writes=[c_sm]))
        setup.append(op(POOL, lambda e: e.memset(eps_t[:], EPS)))
        c_rm = Res("rm")
        setup.append(op(POOL, lambda e: e.memset(rowmask[:], 0.0), writes=[c_rm]))
        setup.append(op(POOL, lambda e: e.memset(rowmask[0:32, 0:1], 1.0), writes=[c_rm]))
        setup.append(op(POOL, lambda e: e.memset(rowmask[32:64, 1:2], 1.0), writes=[c_rm]))
        setup.append(op(POOL, lambda e: e.memset(Ucarry[:], 0.0)))
        setup.append(op(POOL, lambda e: e.memset(ebLcarry[:], 1.0)))
        setup.append(op(POOL, lambda e: e.memset(zcarry[:], 0.0)))
        setup.append(op(POOL, lambda e: e.memset(ss[:], 0.0)))
        setup.append(op(SP, lambda e: e.dma_start(
            out=lbraw[:], in_=lb_hgrn.rearrange("r (hc p) -> p r hc", p=128),
            allow_slow_non_contiguous=True), writes=[c_lbraw], dsem=misc_sem))
        setup.append(op(SP, lambda e: e.dma_start(
            out=gnorm[:], in_=g_norm_hgrn[0, :].rearrange("(hc p) -> p hc", p=128),
            allow_slow_non_contiguous=True), dsem=misc_sem))
        setup.append(op(SP, lambda e: e.dma_start(
            out=cwt[:], in_=conv_w.rearrange("j (fc p) -> p j fc", p=128),
            allow_slow_non_contiguous=True), dsem=misc_sem))
        for q in range(2):
            setup.append(op(SP, lambda e, q=q: e.dma_start(
                out=shist[:, q, :, :], in_=s_cv[q].rearrange("r (fc p) -> p r fc", p=128),
                allow_slow_non_contiguous=True), dsem=misc_sem))
        early_loads()
        setup = [ev for ev in setup if ev[0] is not misc_sem] + [(misc_sem, misc_sem.v)]
        for q in queues:
            _wait(q, setup)
        setup2 = []
        setup2.append(op(DVE, lambda e: e.tensor_copy(ident[:], identf)))
        setup2.append(op(DVE, lambda e: e.tensor_copy(tri4[:, :, :], trif), writes=[c_tri]))
        setup2.append(op(DVE, lambda e: e.tensor_copy(msk_s[0:32, 0:32], trif[0:32, 0, 0:32]), writes=[c_msk]))
        setup2.append(op(DVE, lambda e: e.tensor_copy(msk_s[32:64, 32:64], trif[32:64, 0, 32:64]), writes=[c_msk]))
        setup2.append(op(DVE, lambda e: e.tensor_sub(lbm1[:], lbraw[:, 1, :], lbraw[:, 0, :]), reads=[c_lbraw], writes=[c_lb]))
        setup2.append(op(ACT, lambda e: e.activation(out=lbm1[:], in_=lbm1[:], func=AF.Exp), reads=[c_lb], writes=[c_lb]))
        setup2.append(op(ACT, lambda e: e.activation(out=lbm1[:], in_=lbm1[:], func=AF.Ln, bias=1.0), reads=[c_lb], writes=[c_lb]))
        setup2.append(op(ACT, lambda e: e.activation(out=oml[:], in_=lbm1[:], func=AF.Exp, scale=-1.0), reads=[c_lb], writes=[c_oml]))
        setup2.append(op(DVE, lambda e: e.tensor_scalar_add(lbm1[:], oml[:], -1.0), reads=[c_oml], writes=[c_lb]))
        setup2.append(op(DVE, lambda e: e.tensor_scalar(oml[:], oml[:], -1.0, 1.0, ALU.mult, ALU.add), reads=[c_lb, c_oml], writes=[c_oml]))
        for q in queues:
            _wait(q, setup2)

        plan = []
        for t in range(NT):
            for hd in range(8):
                plan.append(("hg", hd))
            for oh in range(2):
                plan.append(("wo", 0, oh))
            for g in range(4):
                plan += [("up", 0, g, 0), ("up", 0, g, 1), ("dn", 0, g, 0), ("dn", 0, g, 1)]
            for fc in range(8):
                plan.append(("cv", fc))
            for oh in range(2):
                plan.append(("wo", 1, oh))
            for g in range(4):
                plan += [("up", 1, g, 0), ("up", 1, g, 1), ("dn", 1, g, 0), ("dn", 1, g, 1)]
        wstate = {"issued": 0, "next": 0}

        def w_src_dst(tag, slot):
            k = tag[0]
            if k == "hg":
                sv = w_in_hgrn.rearrange("(kc p) (t hh j) -> p kc t hh j", p=128, t=4, hh=8)
                dv = slot[:, 0:4096].rearrange("p (kc t j) -> p kc t j", kc=8, t=4)
                return [(sv[:, :, ty, tag[1], :], dv[:, :, ty, :]) for ty in range(4)]
            elif k == "cv":
                sv = w_in_conv.rearrange("(kc p) (t ff j) -> p kc t ff j", p=128, t=3, ff=8)
                dv = slot[:, 0:3072].rearrange("p (kc t j) -> p kc t j", kc=8, t=3)
                return [(sv[:, :, ty, tag[1], :], dv[:, :, ty, :]) for ty in range(3)]
            elif k == "wo":
                wsrc = w_out_hgrn if tag[1] == 0 else w_out_conv
                src = wsrc.rearrange("(kc p) n -> p kc n", p=128)[:, :, tag[2] * 512:(tag[2] + 1) * 512]
            elif k == "up":
                _, l, g, j = tag
                c0 = g * 1024 + j * 512
                src = w_up[l].rearrange("(kc p) n -> p kc n", p=128)[:, :, c0:c0 + 512]
            else:
                _, l, g, oh = tag
                src = w_down[l][g * 1024:(g + 1) * 1024, :].rearrange("(kc p) n -> p kc n", p=128)[:, :, oh * 512:(oh + 1) * 512]
            dst = slot[:, 0:4096].rearrange("p (kc n) -> p kc n", kc=8)
            return [(src, dst)]

        def w_issue_upto(n):
            while wstate["issued"] <= min(n, len(plan) - 1):
                m = wstate["issued"]
                s = m % NSLOT
                pairs = w_src_dst(plan[m], wring[s])
                _wait(POOL, _deps([], [slot_res[s]]))
                for (src, dst) in pairs:
                    ins = POOL.eng.dma_start(out=dst, in_=src)
                    slot_sem[s].v += 16
                    ins.then_inc(slot_sem[s].h, 16)
                _stamp(POOL, (slot_sem[s], slot_sem[s].v))
                _record((slot_sem[s], slot_sem[s].v), [], [slot_res[s]])
                wstate["issued"] += 1

        def w_acquire(tag, ahead=NSLOT - 1):
            n = wstate["next"]
            assert plan[n] == tag, (plan[n], tag)
            wstate["next"] += 1
            w_issue_upto(n + ahead)
            s = n % NSLOT
            return wring[s], slot_res[s]

        w_issue_upto(NSLOT - 2)

        def tile_geom(t):
            blocks = [(b, b * 128, 128) for b in range(8)]
            cts = [(0, 512, [0, 1, 2, 3]), (512, 512, [4, 5, 6, 7])]
            if t == 0:
                blocks.append((8, TP, 64))
                cts.append((TP, 64, [8]))
            return blocks, cts

        def load_x(t, blocks):
            for (b, c0, ntok) in blocks:
                if b < 8:
                    src = xp[t * TP + b * 128: t * TP + (b + 1) * 128, :]
                else:
                    src = xs[:, :]
                op(SP, lambda e, b=b, ntok=ntok, src=src: e.dma_start(out=h[0:ntok, b, :], in_=src),
                   writes=[h_res[b]], dsem=h_sem[b])

        gam_ctr = [0]

        ss_r2 = [[Res("ss%d_%d" % (i, b)) for b in range(9)] for i in range(2)]
        rstd_r2 = [[Res("rstd%d_%d" % (i, b)) for b in range(9)] for i in range(2)]

        def norm_begin(gamma_row, sset=0):
            gi = gam_ctr[0] % 2
            gam_ctr[0] += 1
            op(SP, lambda e: e.dma_start(out=gam[gi][:], in_=gamma_row.broadcast_to([128, D])),
               writes=[gam_res[gi]], dsem=gam_sem[gi])
            op(POOL, lambda e: e.memset(ss2[sset][:], 0.0), writes=ss_r2[sset])
            return gi

        def norm_stats_block(b, ntok, ai=0, sset=0, junk=None):
            sst, rst = ss2[sset], rstd2[sset]
            jt, jr = junk if junk is not None else (aTM[ai], aTM_res[ai])
            op(ACT, lambda e: e.activation(
                out=jt[0:ntok, :], in_=h[0:ntok, b, :], func=AF.Square, accum_out=sst[0:ntok, b:b + 1]),
               reads=[h_res[b]], writes=[jr, ss_r2[sset][b]])
            op(ACT, lambda e: e.activation(out=rst[0:ntok, b:b + 1], in_=sst[0:ntok, b:b + 1], func=AF.Ln,
                                           scale=1.0 / D, bias=eps_t[0:ntok, :]),
               reads=[ss_r2[sset][b]], writes=[rstd_r2[sset][b]])
            op(ACT, lambda e: e.activation(out=rst[0:ntok, b:b + 1], in_=rst[0:ntok, b:b + 1], func=AF.Exp, scale=-0.5),
               reads=[rstd_r2[sset][b]], writes=[rstd_r2[sset][b]])

        atm_ctr = [0]

        def norm_p1(b, c0, ntok, gi, sset=0):
            ai = atm_ctr[0] % NATM
            atm_ctr[0] += 1
            norm_stats_block(b, ntok, ai, sset)
            op(DVE, lambda e: e.scalar_tensor_tensor(
                aTM[ai][0:ntok, :], h[0:ntok, b, :], rstd2[sset][0:ntok, b:b + 1], gam[gi][0:ntok, :],
                ALU.mult, ALU.mult),
               reads=[h_res[b], rstd_r2[sset][b], gam_res[gi]], writes=[aTM_res[ai]])
            return ai

        def norm_p2(b, c0, ntok, ai):
            tb, tr = tp_next()
            pe_group([lambda e, kc=kc: e.transpose(
                tb[:, kc, 0:ntok], aTM[ai][0:ntok, kc * 128:(kc + 1) * 128], ident[0:ntok, 0:ntok])
                for kc in range(KC)], reads=[aTM_res[ai]], writes=[tr])
            op(ACT, lambda e: e.copy(aFM[:, :, c0:c0 + ntok], tb[:, :, 0:ntok]),
               reads=[tr], writes=[a_res[b]])

        def norm_to_fm(blocks, gamma_row, gi=None):
            if gi is None:
                gi = norm_begin(gamma_row)
            else:
                op(POOL, lambda e: e.memset(ss2[0][:], 0.0), writes=ss_r2[0])
            prev = None
            for (b, c0, ntok) in blocks:
                ai = norm_p1(b, c0, ntok, gi)
                if prev is not None:
                    norm_p2(*prev)
                prev = (b, c0, ntok, ai)
            norm_p2(*prev)

        def tm_proj_add(blocks, act_res_of_blk, wtag_fn, stages=None, lags=None, begin_final=None):
            def one(slot_wv, sres, oh, b, c0, ntok):
                bank, bres = mm_next()
                pe_group([lambda e, kc=kc: e.matmul(
                    bank[0:ntok, :], hid[:, kc, c0:c0 + ntok], slot_wv[:, kc, :],
                    start=(kc == 0), stop=(kc == KC - 1)) for kc in range(KC)],
                    reads=[sres, act_res_of_blk(b)], writes=[bres])
                op(DVE, lambda e: e.tensor_tensor(
                    h[0:ntok, b, oh * 512:(oh + 1) * 512], h[0:ntok, b, oh * 512:(oh + 1) * 512],
                    bank[0:ntok, :], ALU.add),
                   reads=[bres, h_res[b]], writes=[h_res[b]])
            if stages is None:
                for oh in range(2):
                    slot, sres = w_acquire(wtag_fn(oh))
                    wv = slot[:, 0:4096].rearrange("p (kc n) -> p kc n", kc=8)
                    for (b, c0, ntok) in blocks:
                        one(wv, sres, oh, b, c0, ntok)
                return
            ws = []
            for oh in range(2):
                slot, sres = w_acquire(wtag_fn(oh), ahead=NSLOT - 1 - oh)
                ws.append((slot[:, 0:4096].rearrange("p (kc n) -> p kc n", kc=8), sres))
            if begin_final is not None:
                begin_final()
            K = len(stages)
            queues_ = [[] for _ in range(K)]

            def step(final=False):
                for k in reversed(range(K)):
                    if queues_[k] and (final or len(queues_[k]) > lags[k]):
                        r = stages[k](*queues_[k].pop(0))
                        if k + 1 < K and r is not None:
                            queues_[k + 1].append(r)
            for (b, c0, ntok) in blocks:
                for oh in range(2):
                    one(ws[oh][0], ws[oh][1], oh, b, c0, ntok)
                queues_[0].append((b, c0, ntok))
                step()
            while any(queues_):
                step(final=True)

        rl_ctr = [0]

        def mlp(l, blocks, cts, stages=None, lags=None, begin_final=None):
            ct_of_blk = {}
            for ci, (c0, cw, bl) in enumerate(cts):
                for b in bl:
                    ct_of_blk[b] = ci
            for g in range(4):
                for j in range(2):
                    slot, sres = w_acquire(("up", l, g, j))
                    wv = slot[:, 0:4096].rearrange("p (kc n) -> p kc n", kc=8)
                    for ocl in range(4):
                        oc = j * 4 + ocl
                        for ci, (c0, cw, bl) in enumerate(cts):
                            bank, bres = mm_next()
                            pe_group([lambda e, kc=kc, c0=c0, cw=cw, bank=bank, wv=wv, ocl=ocl: e.matmul(
                                bank[:, 0:cw], wv[:, kc, ocl * 128:(ocl + 1) * 128], aFM[:, kc, c0:c0 + cw],
                                start=(kc == 0), stop=(kc == KC - 1)) for kc in range(KC)],
                                reads=[sres] + [a_res[b] for b in bl], writes=[bres])
                            ri = rl_ctr[0] % 2
                            rl_ctr[0] += 1
                            op(ACT, lambda e, cw=cw, bank=bank, ri=ri: e.activation(
                                out=rl[ri][:, 0:cw], in_=bank[:, 0:cw], func=AF.Relu),
                               reads=[bres], writes=[rl_res[ri]])
                            op(DVE, lambda e, cw=cw, c0=c0, oc=oc, ri=ri: e.tensor_tensor(
                                hid[:, oc, c0:c0 + cw], rl[ri][:, 0:cw], rl[ri][:, 0:cw], ALU.mult),
                               reads=[rl_res[ri]], writes=[hid_res[ci]])
                tm_proj_add(blocks, lambda b: hid_res[ct_of_blk[b]], lambda oh: ("dn", l, g, oh),
                            stages=(stages if g == 3 else None), lags=lags, begin_final=(begin_final if g == 3 else None))

        fence = []
        yo_ctr = [0]
        for t in range(NT):
            blocks, cts = tile_geom(t)
            ct_of_blk = {}
            for ci, (c0, cw, bl) in enumerate(cts):
                for b in bl:
                    ct_of_blk[b] = ci
            nb = len(blocks)
            has_s = (t == 0)

            nstate = {}

            def mk_hooks(gamma_row):
                def begin():
                    nstate["gi"] = norm_begin(gamma_row)

                def p1(b, c0, ntok):
                    return (b, c0, ntok, norm_p1(b, c0, ntok, nstate["gi"]))

                def p2(b, c0, ntok, ai):
                    norm_p2(b, c0, ntok, ai)
                return [p1, p2], begin
            if t == 0:
                gam_ctr[0] = 1
                norm_to_fm(blocks, norm_mix[0:1, :], gi=early["gi"])
            with ExitStack() as hs:
                def hsb(name, shape, dt):
                    return sb(name + "_t%d" % t, shape, dt, stack=hs)
                newres = []

                def R(name):
                    r = Res(name, pre=fence)
                    newres.append(r)
                    return r
                qt = [hsb("qt%d" % i, [128, TC], BF16) for i in range(2)]
                kt = [hsb("kt%d" % i, [128, TC], BF16) for i in range(2)]
                sgg = [hsb("sgg%d" % i, [128, TC], F32) for i in range(2)]
                vT = [hsb("vT%d" % i, [128, 9, 128], BF16) for i in range(2)]
                ebL = [hsb("ebL%d" % i, [128, 12], F32) for i in range(2)]
                gt = [[hsb("g%s%d" % (n, i), [128, 512], F32) for n in "abcd"] for i in range(2)]
                ktT = hsb("ktT", [128, 9, 128], BF16)
                ATm = hsb("ATm", [128, 9, 128], BF16)
                Sbf = hsb("Sbf", [128, 8, 128], BF16)
                Ubuf = hsb("Ubuf", [128, 9, 128], F32)
                osq = [hsb("osq%d" % i, [128, 512], BF16) for i in range(2)]
                orr = [hsb("orr%d" % i, [128, 512], F32) for i in range(2)]
                qt_r = [[R("qt") for _ in range(3)] for i in range(2)]
                kt_r = [[R("kt") for _ in range(3)] for i in range(2)]
                sgg_r = [[R("sgg") for _ in range(3)] for i in range(2)]
                vT_r = [[R("vT") for _ in range(3)] for i in range(2)]
                ebL_r = [R("ebL") for i in range(2)]
                gt_r = [[R("g") for n in "abcd"] for i in range(2)]
                ktT_r = [R("ktT") for _ in range(3)]
                ATm_r = [R("ATm") for _ in range(3)]
                Sbf_r = [R("Sbf") for _ in range(8)]
                U_r = [R("U") for _ in range(9)]
                osq_r = [R("osq") for i in range(2)]
                orr_r = [R("orr") for i in range(2)]
                if has_s:
                    S0 = hsb("S0", [128, 2, 8, 128], F32)
                    S0bf = hsb("S0bf", [128, 2, 8, 128], BF16)
                    Us = hsb("Us", [128, 2, 128], F32)
                    ktTs = hsb("ktTs", [128, 2, 128], BF16)
                    S0_r = [[R("S0") for _ in range(8)] for q in range(2)]
                    S0bf_r = R("S0bf")
                    Us_r = R("Us")
                    for q in range(2):
                        op(SP, lambda e, q=q: e.dma_start(out=S0[:, q, :, :], in_=s_hg[q].rearrange("h k v -> k h v")),
                           writes=S0_r[q], dsem=s0_sem[q])
                    op(ACT, lambda e: e.copy(S0bf[:].rearrange("p q h v -> p (q h v)"),
                                             S0[:].rearrange("p q h v -> p (q h v)")),
                       reads=S0_r[0] + S0_r[1], writes=[S0bf_r])
                gctr = [0]

                def H1(hd):
                    S = hd % 2
                    st = {}

                    def fm(typ, c0, cw, bl, hold=False):
                        bank, bres = mm_next(hold)
                        wv, sres = st["wv"], st["sres"]
                        pe_group([lambda e, kc=kc: e.matmul(
                            bank[:, 0:cw], wv[:, kc, typ, :], aFM[:, kc, c0:c0 + cw],
                            start=(kc == 0), stop=(kc == KC - 1)) for kc in range(KC)],
                            reads=[sres] + [a_res[b] for b in bl], writes=[bres])
                        return bank, bres

                    def mkA(ci, c0, cw, bl):
                        def A_pe():
                            if "wv" not in st:
                                slot, sres = w_acquire(("hg", hd))
                                st["wv"] = slot[:, 0:4096].rearrange("p (kc t j) -> p kc t j", kc=8, t=4)
                                st["sres"] = sres
                            gi = gctr[0] % 2
                            gctr[0] += 1
                            st[ci] = gi
                            st[("f", ci)] = fm(1, c0, cw, bl, hold=True)

                        def A_act():
                            gi = st[ci]
                            tA, tS, tL, tB = [x[:, 0:cw] for x in gt[gi]]
                            rA, rS, rL, rB = gt_r[gi]
                            bf, bf_r = st[("f", ci)]
                            op(ACT, lambda e: e.activation(out=tA, in_=bf[:, 0:cw], func=AF.Exp), reads=[bf_r], writes=[rA])
                            mm_release(bf_r)
                            op(ACT, lambda e: e.activation(out=tA, in_=tA, func=AF.Ln, bias=1.0), reads=[rA], writes=[rA])
                            op(ACT, lambda e: e.activation(out=tS, in_=tA, func=AF.Exp, scale=-1.0), reads=[rA], writes=[rS])
                            op(ACT, lambda e: e.activation(out=tL, in_=tS, func=AF.Ln, scale=lbm1[:, hd:hd + 1], bias=1.0),
                               reads=[rS], writes=[rL])

                        def A():
                            A_pe()
                            A_act()
                        A.pe = A_pe
                        A.act = A_act
                        return A

                    def mkB(ci, c0, cw, bl):
                        def B():
                            gi = st[ci]
                            wv, sres = st["wv"], st["sres"]
                            tA, tS, tL, tB = [x[:, 0:cw] for x in gt[gi]]
                            rA, rS, rL, rB = gt_r[gi]
                            bg, bg_r = fm(3, c0, cw, bl, hold=True)
                            st[("g", ci)] = (bg, bg_r)
                            bank, bres = mm_next()
                            mms = []
                            for j, b in enumerate(bl):
                                ntok = blocks[b][2]
                                bc0 = blocks[b][1]
                                for kc in range(KC):
                                    mms.append(lambda e, kc=kc, j=j, ntok=ntok, bc0=bc0: e.matmul(
                                        bank[0:ntok, j * 128:(j + 1) * 128], aFM[:, kc, bc0:bc0 + ntok], wv[:, kc, 2, :],
                                        start=(kc == 0), stop=(kc == KC - 1)))
                            pe_group(mms, reads=[sres] + [a_res[b] for b in bl], writes=[bres])
                            if cw == 512:
                                op(ACT, lambda e: e.copy(vT[S][:, bl[0]:bl[0] + 4, :],
                                                         bank[:, :].rearrange("p (j v) -> p j v", j=4)),
                                   reads=[bres], writes=[vT_r[S][ci]])
                            else:
                                op(ACT, lambda e: e.copy(vT[S][0:64, 8, :], bank[0:64, 0:128]),
                                   reads=[bres], writes=[vT_r[S][ci]])
                            bq, bq_r = fm(0, c0, cw, bl)
                            op(DVE, lambda e: e.tensor_tensor_scan(tB, smask[:, c0:c0 + cw], tL, 0.0, ALU.mult, ALU.add),
                               reads=[rL], writes=[rB])
                            op(ACT, lambda e: e.activation(out=tA, in_=tB, func=AF.Exp), reads=[rB], writes=[rA])
                            op(ACT, lambda e: e.activation(out=tL, in_=tB, func=AF.Exp, scale=-1.0), reads=[rB], writes=[rL])
                            op(DVE, lambda e: e.tensor_tensor(qt[S][:, c0:c0 + cw], bq[:, 0:cw], tA, ALU.mult),
                               reads=[bq_r, rA], writes=[qt_r[S][ci]])
                            op(DVE, lambda e: e.scalar_tensor_tensor(kt[S][:, c0:c0 + cw], tS, oml[:, hd:hd + 1], tL,
                                                                     ALU.mult, ALU.mult),
                               reads=[rS, rL], writes=[kt_r[S][ci]])
                            if cw == 512:
                                op(DVE, lambda e: e.tensor_copy(ebL[S][:, 1 + 4 * ci:5 + 4 * ci],
                                                                tA.rearrange("p (c l) -> p c l", l=128)[:, :, 127]),
                                   reads=[rA], writes=[ebL_r[S]])
                            else:
                                op(DVE, lambda e: e.tensor_copy(ebL[S][:, 9:11],
                                                                tA.rearrange("p (c l) -> p c l", l=32)[:, :, 31]),
                                   reads=[rA], writes=[ebL_r[S]])
                        return B

                    def mkC(ci, c0, cw, bl):
                        def C():
                            gi = st[ci]
                            tA, tS, tL, tB = [x[:, 0:cw] for x in gt[gi]]
                            rA, rS, rL, rB = gt_r[gi]
                            bg, bg_r = st[("g", ci)]
                            op(ACT, lambda e: e.activation(out=tB, in_=bg[:, 0:cw], func=AF.Exp, scale=-1.0),
                               reads=[bg_r], writes=[rB])
                            op(ACT, lambda e: e.activation(out=tB, in_=tB, func=AF.Ln, bias=1.0), reads=[rB], writes=[rB])
                            op(ACT, lambda e: e.activation(out=tB, in_=tB, func=AF.Exp, scale=-1.0), reads=[rB], writes=[rB])
                            op(DVE, lambda e: e.tensor_tensor(sgg[S][:, c0:c0 + cw], bg[:, 0:cw], tB, ALU.mult),
                               reads=[bg_r, rB], writes=[sgg_r[S][ci]])
                            mm_release(bg_r)
                        return C
                    return [(mkA(ci, c0, cw, bl), mkB(ci, c0, cw, bl), mkC(ci, c0, cw, bl))
                            for ci, (c0, cw, bl) in enumerate(cts)]

                def H2(hd):
                    S = hd % 2
                    st = {}

                    def a():
                        for ci, (c0, cw, bl) in enumerate(cts):
                            ab, ab_r = mm_next(hold=True)
                            st[("A", ci)] = (ab, ab_r)
                            if cw == 512:
                                abv = v4(ab)
                                pe_group([lambda e, j=j: e.matmul(
                                    abv[:, j, :], kt[S][:, c0 + j * 128:c0 + (j + 1) * 128],
                                    qt[S][:, c0 + j * 128:c0 + (j + 1) * 128], start=True, stop=True) for j in range(4)],
                                    reads=[kt_r[S][ci], qt_r[S][ci]], writes=[ab_r])
                            else:
                                pe_group([lambda e: e.matmul(ab[0:64, 0:64], kt[S][:, c0:c0 + 64], qt[S][:, c0:c0 + 64],
                                                             start=True, stop=True)],
                                         reads=[kt_r[S][ci], qt_r[S][ci]], writes=[ab_r])
                        tb, tr = tp_next(hold=True)
                        st["T"] = (tb, tr)
                        pe_group([lambda e, j=j: e.transpose(
                            tb[:, j, :], kt[S][:, j * 128:(j + 1) * 128], ident[:, :]) for j in range(8)],
                            reads=[kt_r[S][0], kt_r[S][1]], writes=[tr])
                        if has_s:
                            tb2, tr2 = tp_next(hold=True)
                            st["T2"] = (tb2, tr2)
                            pe_group([lambda e: e.transpose(tb2[0:64, 0, :], kt[S][:, TP:TP + 64], ident[:, :])],
                                     reads=[kt_r[S][2]], writes=[tr2])

                    def b():
                        op(DVE, lambda e: e.tensor_copy(Ubuf[:, 0, :], Ucarry[:, hd, :]), reads=[ucarry_res[hd]], writes=[U_r[0]])
                        op(DVE, lambda e: e.tensor_copy(ebL[S][:, 0:1], ebLcarry[:, hd:hd + 1]),
                           reads=[eblc_res[hd]], writes=[ebL_r[S]])
                        for ci, (c0, cw, bl) in enumerate(cts):
                            ab, ab_r = st[("A", ci)]
                            if cw == 512:
                                op(DVE, lambda e: e.tensor_tensor(ATm[:, bl[0]:bl[0] + 4, :], v4(ab), tri4[:, :, :], ALU.mult),
                                   reads=[ab_r], writes=[ATm_r[ci]])
                            else:
                                op(DVE, lambda e: e.tensor_tensor(ATm[0:64, 8, 0:64], ab[0:64, 0:64], msk_s[0:64, 0:64], ALU.mult),
                                   reads=[ab_r], writes=[ATm_r[ci]])
                            mm_release(ab_r)
                        tb, tr = st["T"]
                        op(ACT, lambda e: e.copy(ktT[:, 0:8, :], tb[:, 0:8, :]), reads=[tr], writes=[ktT_r[0], ktT_r[1]])
                        mm_release(tr)
                        if has_s:
                            tb2, tr2 = st["T2"]
                            for q in range(2):
                                op(ACT, lambda e, q=q: e.activation(out=ktTs[0:64, q, :], in_=tb2[0:64, 0, :], func=AF.Identity,
                                                                    scale=rowmask[0:64, q:q + 1]),
                                   reads=[tr2], writes=[ktT_r[2]])
                            mm_release(tr2)

                    def c():
                        for ci, (c0, cw, bl) in enumerate(cts):
                            kb, kb_r = mm_next()
                            kbv = v4(kb)
                            if cw == 512:
                                pe_group([lambda e, j=j: e.matmul(
                                    kbv[:, j, :], ktT[:, bl[0] + j, :], vT[S][:, bl[0] + j, :], start=True, stop=True)
                                    for j in range(4)], reads=[ktT_r[ci], vT_r[S][ci]], writes=[kb_r])
                                for j in range(4):
                                    cix = bl[0] + j
                                    op(DVE, lambda e, cix=cix, j=j: e.scalar_tensor_tensor(
                                        Ubuf[:, cix + 1, :], Ubuf[:, cix, :], ebL[S][:, cix:cix + 1], kbv[:, j, :], ALU.mult, ALU.add),
                                       reads=[U_r[cix], ebL_r[S], kb_r], writes=[U_r[cix + 1]])
                            else:
                                pe_group([lambda e, q=q: e.matmul(
                                    kbv[:, q, :], ktTs[0:64, q, :], vT[S][0:64, 8, :],
                                    start=True, stop=True) for q in range(2)], reads=[ktT_r[ci], vT_r[S][ci]], writes=[kb_r])
                                for q in range(2):
                                    op(DVE, lambda e, q=q: e.tensor_tensor(Us[:, q, :], S0[:, q, hd, :], kbv[:, q, :], ALU.add),
                                       reads=[S0_r[q][hd], kb_r], writes=[Us_r])
                                    op(ACT, lambda e, q=q: e.activation(out=S0[:, q, hd, :], in_=Us[:, q, :], func=AF.Identity,
                                                                        scale=ebL[S][:, 9 + q:10 + q]),
                                       reads=[Us_r, ebL_r[S], S0bf_r], writes=[S0_r[q][hd]])

                    def c2():
                        op(DVE, lambda e: e.tensor_tensor(
                            Sbf[:, 0:8, :], Ubuf[:, 0:8, :],
                            ebL[S][:, 0:8].unsqueeze(2).broadcast_to([128, 8, 128]), ALU.mult),
                           reads=U_r[0:8] + [ebL_r[S]], writes=Sbf_r)

                    def d():
                        op(DVE, lambda e: e.tensor_copy(Ucarry[:, hd, :], Ubuf[:, 8, :]), reads=[U_r[8]], writes=[ucarry_res[hd]])
                        op(DVE, lambda e: e.tensor_copy(ebLcarry[:, hd:hd + 1], ebL[S][:, 8:9]),
                           reads=[ebL_r[S]], writes=[eblc_res[hd]])
                        for ci in range(2):
                            d_ct(ci)

                    def d_ct(ci):
                        if True:
                            c0, cw, bl = cts[ci]
                            bo, bo_r = mm_next(hold=True)
                            st[("o", ci)] = (bo, bo_r)
                            mms = []
                            if cw == 512:
                                for j in range(4):
                                    cix = bl[0] + j
                                    cc0 = c0 + j * 128
                                    mms.append(lambda e, j=j, cix=cix: e.matmul(
                                        bo[:, j * 128:(j + 1) * 128], vT[S][:, cix, :], ATm[:, cix, :], start=True, stop=False))
                                    mms.append(lambda e, j=j, cix=cix, cc0=cc0: e.matmul(
                                        bo[:, j * 128:(j + 1) * 128], Sbf[:, cix, :], qt[S][:, cc0:cc0 + 128], start=False, stop=True))
                                r2 = [Sbf_r[bl[0] + j] for j in range(4)] + [qt_r[S][ci]]
                            else:
                                for q in range(2):
                                    mms.append(lambda e, q=q: e.matmul(
                                        bo[:, 32 * q:32 * q + 32], vT[S][0:64, 8, :], ATm[0:64, 8, 32 * q:32 * q + 32],
                                        start=True, stop=False))
                                    mms.append(lambda e, q=q: e.matmul(
                                        bo[:, 32 * q:32 * q + 32], S0bf[:, q, hd, :], qt[S][:, c0 + 32 * q:c0 + 32 * q + 32],
                                        start=False, stop=True))
                                r2 = [S0bf_r, qt_r[S][ci]]
                            pe_group(mms, reads=[vT_r[S][ci], ATm_r[ci]] + r2, writes=[bo_r])

                    def e_():
                        for ci in range(2):
                            e_ct(ci)

                    def e_ct(ci):
                        if True:
                            c0, cw, bl = cts[ci]
                            bo, bo_r = st[("o", ci)]
                            oi = ci % 2
                            op(ACT, lambda e: e.activation(out=osq[oi][:, 0:cw], in_=bo[:, 0:cw], func=AF.Square),
                               reads=[bo_r], writes=[osq_r[oi]])
                            bs, bs_r = mm_next(hold=True)
                            st[("s", ci)] = (bs, bs_r)
                            pe_group([lambda e: e.matmul(bs[:, 0:cw], ones_bf[:, :], osq[oi][:, 0:cw], start=True, stop=True)],
                                     reads=[osq_r[oi]], writes=[bs_r])

                    def f_():
                        for ci in range(2):
                            f_ct(ci)
                        if has_s:
                            d_ct(2)
                            e_ct(2)
                            f_ct(2)

                    def f_ct(ci):
                        if True:
                            c0, cw, bl = cts[ci]
                            bo, bo_r = st[("o", ci)]
                            bs, bs_r = st[("s", ci)]
                            oi = ci % 2
                            op(ACT, lambda e: e.activation(out=orr[oi][:, 0:cw], in_=bs[:, 0:cw], func=AF.Ln,
                                                           scale=1.0 / 128, bias=eps_t[:]),
                               reads=[bs_r], writes=[orr_r[oi]])
                            op(ACT, lambda e: e.activation(out=orr[oi][:, 0:cw], in_=orr[oi][:, 0:cw], func=AF.Exp, scale=-0.5),
                               reads=[orr_r[oi]], writes=[orr_r[oi]])
                            op(DVE, lambda e: e.tensor_tensor(orr[oi][:, 0:cw], orr[oi][:, 0:cw], sgg[S][:, c0:c0 + cw], ALU.mult),
                               reads=[orr_r[oi], sgg_r[S][ci]], writes=[orr_r[oi]])
                            op(DVE, lambda e: e.scalar_tensor_tensor(hid[:, hd, c0:c0 + cw], bo[:, 0:cw], gnorm[:, hd:hd + 1],
                                                                     orr[oi][:, 0:cw], ALU.mult, ALU.mult),
                               reads=[bo_r, orr_r[oi]], writes=[hid_res[ci]])
                            mm_release(bo_r)
                            mm_release(bs_r)
                    def cc_():
                        c()
                        c2()
                    return [a, b, cc_, d, e_, f_]

                h1s = {hd: H1(hd) for hd in range(8)}
                h2s = {hd: H2(hd) for hd in range(8)}

                def n_(P, ci, k, part=None):
                    hd = P + 1
                    if 0 <= hd < 8 and ci < len(h1s[hd]):
                        fn = h1s[hd][ci][k]
                        if part is not None:
                            fn = getattr(fn, part)
                        fn()

                def p_(P, k):
                    if 0 <= P < 8:
                        h2s[P][k]()
                for P in range(-1, 9):
                    p_(P - 1, 4)
                    n_(P, 0, 0, "pe")
                    p_(P - 1, 5)
                    n_(P, 0, 0, "act")
                    n_(P, 0, 1)
                    p_(P, 0)
                    n_(P, 0, 2)
                    p_(P, 1)
                    p_(P, 2)
                    n_(P, 1, 0)
                    n_(P, 1, 1)
                    p_(P, 3)
                    n_(P, 1, 2)
                    for k in range(3):
                        n_(P, 2, k)
                if has_s:
                    for q in range(2):
                        op(SP, lambda e, q=q: e.dma_start(out=hgs[q].rearrange("h k v -> k h v"), in_=S0[:, q, :, :]),
                           reads=S0_r[q], dsem=out_sem)
                if t == NT - 1:
                    fin = sgg[0][:, 0:1024].rearrange("p (h v) -> p h v", h=8)
                    for hd in range(8):
                        op(ACT, lambda e, hd=hd: e.activation(out=fin[:, hd, :], in_=Ucarry[:, hd, :], func=AF.Identity,
                                                              scale=ebLcarry[:, hd:hd + 1]),
                           reads=[ucarry_res[hd], eblc_res[hd]] + sgg_r[0], writes=sgg_r[0])
                    op(SP, lambda e: e.dma_start(out=hgp.rearrange("h k v -> k h v"), in_=fin), reads=sgg_r[0], dsem=out_sem)
                fin, beg = mk_hooks(norm_mlp[0:1, :])
                tm_proj_add(blocks, lambda b: hid_res[ct_of_blk[b]], lambda oh: ("wo", 0, oh), stages=fin, lags=[1, 1], begin_final=beg)
                fence = []
                for r in newres:
                    fence.extend(r.w)
                    fence.extend(r.r.values())
                best = {}
                for (sm, v) in fence:
                    if id(sm) not in best or best[id(sm)][1] < v:
                        best[id(sm)] = (sm, v)
                fence = list(best.values())
            fin, beg = mk_hooks(norm_mix[1:2, :])
            mlp(0, blocks, cts, stages=fin, lags=[1, 1], begin_final=beg)

            with ExitStack() as cs:
                def csb(name, shape, dt):
                    return sb(name + "_t%d" % t, shape, dt, stack=cs)
                newres = []

                def R(name):
                    r = Res(name, pre=fence)
                    newres.append(r)
                    return r
                ZW = TP + 2 + 68
                zb = [csb("zb%d" % i, [128, ZW], F32) for i in range(2)]
                cc = [csb("cc%d" % i, [128, ZW], F32) for i in range(2)]
                cgs = [csb("cgs%d" % i, [128, 512], F32) for i in range(2)]
                bgs = [csb("bgs%d" % i, [128, TC], F32) for i in range(2)]
                zb_r = [R("zb") for i in range(2)]
                cc_r = [R("cc") for i in range(2)]
                cgs_r = [R("cgs") for i in range(2)]
                bgs_r = [R("bgs") for i in range(2)]
                if has_s:
                    scv = csb("scv", [128, 2, 2, 8], F32)
                    scv_r = R("scv")
                cg_ctr = [0]
                L = (TP + 68) if has_s else TP
                for fc in range(8):
                    Z = fc % 2
                    slot, sres = w_acquire(("cv", fc))
                    wv = slot[:, 0:3072].rearrange("p (kc t j) -> p kc t j", kc=8, t=3)
                    op(DVE, lambda e: e.tensor_copy(zb[Z][:, 0:2], zcarry[:, :, fc]), reads=[zcarry_res[fc]], writes=[zb_r[Z]])
                    if has_s:
                        op(DVE, lambda e: e.tensor_copy(
                            zb[Z][:, TP + 2:TP + 70].rearrange("p (q l) -> p q l", q=2)[:, :, 0:2], shist[:, :, :, fc]),
                           reads=[], writes=[zb_r[Z]])

                    def fm(typ, c0, cw, bl):
                        bank, bres = mm_next()
                        pe_group([lambda e, kc=kc: e.matmul(
                            bank[:, 0:cw], wv[:, kc, typ, :], aFM[:, kc, c0:c0 + cw],
                            start=(kc == 0), stop=(kc == KC - 1)) for kc in range(KC)],
                            reads=[sres] + [a_res[b] for b in bl], writes=[bres])
                        return bank, bres
                    for ci, (c0, cw, bl) in enumerate(cts):
                        bc, bc_r = fm(1, c0, cw, bl)
                        bu, bu_r = fm(2, c0, cw, bl)
                        bb, bb_r = fm(0, c0, cw, bl)
                        gi = cg_ctr[0] % 2
                        cg_ctr[0] += 1
                        op(ACT, lambda e: e.copy(cgs[gi][:, 0:cw], bc[:, 0:cw]), reads=[bc_r], writes=[cgs_r[gi]])
                        if cw == 512:
                            op(DVE, lambda e: e.tensor_tensor(zb[Z][:, 2 + c0:2 + c0 + cw], bu[:, 0:cw], cgs[gi][:, 0:cw], ALU.mult),
                               reads=[bu_r, cgs_r[gi]], writes=[zb_r[Z]])
                        else:
                            op(DVE, lambda e: e.tensor_tensor(
                                zb[Z][:, TP + 2:TP + 70].rearrange("p (q l) -> p q l", q=2)[:, :, 2:34],
                                bu[:, 0:64].rearrange("p (q l) -> p q l", q=2),
                                cgs[gi][:, 0:64].rearrange("p (q l) -> p q l", q=2), ALU.mult),
                               reads=[bu_r, cgs_r[gi]], writes=[zb_r[Z]])
                        op(ACT, lambda e: e.copy(bgs[Z][:, c0:c0 + cw], bb[:, 0:cw]), reads=[bb_r], writes=[bgs_r[Z]])
                    op(DVE, lambda e: e.tensor_scalar_mul(cc[Z][:, 0:L], zb[Z][:, 0:L], cwt[:, 0, fc:fc + 1]),
                       reads=[zb_r[Z]], writes=[cc_r[Z]])
                    op(DVE, lambda e: e.scalar_tensor_tensor(cc[Z][:, 0:L], zb[Z][:, 1:L + 1], cwt[:, 1, fc:fc + 1],
                                                             cc[Z][:, 0:L], ALU.mult, ALU.add),
                       reads=[zb_r[Z], cc_r[Z]], writes=[cc_r[Z]])
                    op(DVE, lambda e: e.scalar_tensor_tensor(cc[Z][:, 0:L], zb[Z][:, 2:L + 2], cwt[:, 2, fc:fc + 1],
                                                             cc[Z][:, 0:L], ALU.mult, ALU.add),
                       reads=[zb_r[Z], cc_r[Z]], writes=[cc_r[Z]])
                    op(DVE, lambda e: e.tensor_tensor(hid[:, fc, 0:TP], bgs[Z][:, 0:TP], cc[Z][:, 0:TP], ALU.mult),
                       reads=[bgs_r[Z], cc_r[Z]], writes=[hid_res[0], hid_res[1]])
                    if has_s:
                        op(DVE, lambda e: e.tensor_tensor(
                            hid[:, fc, TP:TC].rearrange("p (q l) -> p q l", q=2),
                            bgs[Z][:, TP:TC].rearrange("p (q l) -> p q l", q=2),
                            cc[Z][:, TP + 2:TP + 70].rearrange("p (q l) -> p q l", q=2)[:, :, 0:32], ALU.mult),
                           reads=[bgs_r[Z], cc_r[Z]], writes=[hid_res[2]])
                        op(DVE, lambda e: e.tensor_copy(
                            scv[:, :, :, fc], zb[Z][:, TP + 2:TP + 70].rearrange("p (q l) -> p q l", q=2)[:, :, 32:34]),
                           reads=[zb_r[Z]], writes=[scv_r])
                    op(DVE, lambda e: e.tensor_copy(zcarry[:, :, fc], zb[Z][:, TP:TP + 2]),
                       reads=[zb_r[Z]], writes=[zcarry_res[fc]])
                if has_s:
                    for q in range(2):
                        op(SP, lambda e, q=q: e.dma_start(out=cvs[q].rearrange("r (fc p) -> p r fc", p=128),
                                                          in_=scv[:, q, :, :], allow_slow_non_contiguous=True),
                           reads=[scv_r], dsem=out_sem)
                if t == NT - 1:
                    op(SP, lambda e: e.dma_start(out=cvp.rearrange("r (fc p) -> p r fc", p=128), in_=zcarry[:, :, :],
                                                 allow_slow_non_contiguous=True),
                       reads=zcarry_res, dsem=out_sem)
                fin, beg = mk_hooks(norm_mlp[1:2, :])
                tm_proj_add(blocks, lambda b: hid_res[ct_of_blk[b]], lambda oh: ("wo", 1, oh), stages=fin, lags=[1, 1], begin_final=beg)
                fence = []
                for r in newres:
                    fence.extend(r.w)
                    fence.extend(r.r.values())
                best = {}
                for (sm, v) in fence:
                    if id(sm) not in best or best[id(sm)][1] < v:
                        best[id(sm)] = (sm, v)
                fence = list(best.values())
            nblocks = tile_geom(t + 1)[0] if t + 1 < NT else []
            nb_by_idx = {b: (b, c0, ntok) for (b, c0, ntok) in nblocks}

            def begin_tail():
                nstate["gf"] = norm_begin(norm_final[0:1, :], sset=0)
                if t + 1 < NT:
                    nstate["gn"] = norm_begin(norm_mix[0:1, :], sset=1)

            def fin_tail(b, c0, ntok):
                gi = nstate["gf"]
                yi = yo_ctr[0] % NYO
                yo_ctr[0] += 1
                norm_stats_block(b, ntok, 0, 0, junk=(yo[yi], yo_res[yi]))
                op(DVE, lambda e: e.scalar_tensor_tensor(
                    yo[yi][0:ntok, :], h[0:ntok, b, :], rstd[0:ntok, b:b + 1], gam[gi][0:ntok, :], ALU.mult, ALU.mult),
                   reads=[h_res[b], rstd_r2[0][b], gam_res[gi]], writes=[yo_res[yi]])
                if b < 8:
                    dst = yp[t * TP + b * 128: t * TP + (b + 1) * 128, :]
                else:
                    dst = ys[:, :]
                op(SP, lambda e: e.dma_start(out=dst, in_=yo[yi][0:ntok, :]),
                   reads=[yo_res[yi]], dsem=yo_sem[yi])
                if b in nb_by_idx:
                    load_x(t + 1, [nb_by_idx[b]])
                    return nb_by_idx[b]
                return None

            def next_p1(b, c0, ntok):
                return (b, c0, ntok, norm_p1(b, c0, ntok, nstate["gn"], sset=1))

            def next_p2(b, c0, ntok, ai):
                norm_p2(b, c0, ntok, ai)
            mlp(1, blocks, cts, stages=[fin_tail, next_p1, next_p2], lags=[0, 2, 1], begin_final=begin_tail)

        for sm in yo_sem + [out_sem]:
            if sm.v > 0:
                SP.eng.wait_ge(sm.h, sm.v)
    return nc


_NC_CACHE = {}


def kernel(x_prompt, x_sample, state_hgrn, state_conv, norm_mix, norm_mlp, norm_final,
           w_in_hgrn, lb_hgrn, g_norm_hgrn, w_out_hgrn, w_in_conv, conv_w, w_out_conv,
           w_up, w_down):
    f = lambda a: np.ascontiguousarray(np.asarray(a, dtype=np.float32))
    x_prompt, x_sample, state_hgrn, state_conv = f(x_prompt), f(x_sample), f(state_hgrn), f(state_conv)
    B, SEQ, _ = x_prompt.shape
    NT = SEQ // TP
    n = 8
    if NT not in _NC_CACHE:
        _NC_CACHE[NT] = build(NT)
    nc = _NC_CACHE[NT]
    shared = {
        "norm_mix": f(norm_mix), "norm_mlp": f(norm_mlp), "norm_final": f(norm_final).reshape(1, D),
        "w_in_hgrn": f(w_in_hgrn)[0], "lb_hgrn": f(lb_hgrn), "g_norm_hgrn": f(g_norm_hgrn).reshape(1, D),
        "w_out_hgrn": f(w_out_hgrn)[0], "w_in_conv": f(w_in_conv)[0], "conv_w": f(conv_w)[0],
        "w_out_conv": f(w_out_conv)[0], "w_up": f(w_up), "w_down": f(w_down),
    }
    in_maps = []
    for c in range(n):
        m = dict(shared)
        m["xp"] = x_prompt[c]
        m["xs"] = x_sample[2 * c:2 * c + 2].reshape(64, D)
        m["s_hg"] = state_hgrn[0, 2 * c:2 * c + 2]
        m["s_cv"] = state_conv[0, 2 * c:2 * c + 2]
        in_maps.append(m)
    res = run_bass_kernel_spmd(nc, in_maps, core_ids=list(range(n)))
    rs = res.results
    y_prompt = np.stack([r["yp"] for r in rs], 0)
    y_sample = np.concatenate([r["ys"].reshape(2, 32, D) for r in rs], 0)
    hg_p = np.stack([r["hgp"] for r in rs], 0)[None]
    cv_p = np.stack([r["cvp"] for r in rs], 0)[None]
    hg_s = np.concatenate([r["hgs"] for r in rs], 0)[None]
    cv_s = np.concatenate([r["cvs"] for r in rs], 0)[None]
    return (y_prompt.astype(np.float32), y_sample.astype(np.float32), hg_p.astype(np.float32),
            cv_p.astype(np.float32), hg_s.astype(np.float32), cv_s.astype(np.float32))
```
